# Optimizing a Trainium2 kernel written in Bass

```python
import math
import jax
import jax.numpy as jnp
from jax import lax
import numpy as np

D_MODEL = 1024
BATCH = 8
SEQ = 2048
DEPTH = 4

GRID_W = 64
CTX_LEN = 256
N_MIXERS = 4
N_CYCLES = DEPTH // N_MIXERS
Q_BLOCK = 128
ROPE_THETA = 10000.0
NORM_EPS = 1e-6
NEG_INF = -1e30

A_HEADS = 8
A_HEAD_DIM = 64
B_HEADS = 16
B_KV_HEADS = 4
B_HEAD_DIM = 64
B_WINDOW = 128
C_HEADS = 8
C_Q_RANK = 384
C_KV_RANK = 256
C_NOPE_DIM = 128
C_ROPE_DIM = 64
C_V_DIM = 128
D_HEADS = 8
D_KV_HEADS = 4
D_HEAD_DIM = 128
D_FF = 2816
CONV_W = 3

kernel_name = 'hybrid_interleaved_diffusion_trunk'


def rms_norm(x, g):
    xf = x.astype(jnp.float32)
    y = xf * lax.rsqrt(jnp.mean(xf * xf, axis=-1, keepdims=True) + NORM_EPS)
    return (y * g.astype(jnp.float32)).astype(x.dtype)


def modulate(x, g, shift, scale):
    return rms_norm(x, g) * (1.0 + scale) + shift


def axial_rope_tables(n_tokens, rot_dim):
    n_rows = n_tokens // GRID_W
    rows = jnp.repeat(jnp.arange(n_rows, dtype=jnp.float32), GRID_W)
    cols = jnp.tile(jnp.arange(GRID_W, dtype=jnp.float32), n_rows)
    n_freq = rot_dim // 4
    inv_freq = ROPE_THETA ** (-jnp.arange(n_freq, dtype=jnp.float32) / n_freq)
    ang = jnp.concatenate([rows[:, None] * inv_freq, cols[:, None] * inv_freq], axis=-1)
    return jnp.cos(ang), jnp.sin(ang)


def apply_rope(x, cos, sin):
    x1, x2 = jnp.split(x, 2, axis=-1)
    c = cos[:, None, :].astype(x.dtype)
    s = sin[:, None, :].astype(x.dtype)
    return jnp.concatenate([x1 * c - x2 * s, x1 * s + x2 * c], axis=-1)


def sweep_query_blocks(fn, q):
    b, s = q.shape[:2]
    nb = s // Q_BLOCK
    qb = jnp.moveaxis(q.reshape(b, nb, Q_BLOCK, *q.shape[2:]), 1, 0)
    out = lax.map(lambda a: fn(a[0], a[1]), (jnp.arange(nb), qb))
    return jnp.moveaxis(out, 0, 1).reshape(b, s, *out.shape[3:])


def gqa_attend(q, k, v, scale, mask=None, sink=None):
    b, nq, h, dh = q.shape
    hkv = k.shape[2]
    g = h // hkv
    qg = q.reshape(b, nq, hkv, g, dh)
    s = jnp.einsum('bqhgd,bkhd->bhgqk', qg, k).astype(jnp.float32) * scale
    if mask is not None:
        s = jnp.where(mask, s, NEG_INF)
    if sink is None:
        p = jax.nn.softmax(s, axis=-1)
    else:
        sink_col = jnp.broadcast_to(sink.astype(jnp.float32).reshape(1, hkv, g, 1, 1), s.shape[:-1] + (1,))
        p = jax.nn.softmax(jnp.concatenate([s, sink_col], axis=-1), axis=-1)[..., :-1]
    o = jnp.einsum('bhgqk,bkhd->bqhgd', p.astype(v.dtype), v)
    return o.reshape(b, nq, h, v.shape[-1])


def merge_heads(o, w_o):
    return o.reshape(o.shape[0], o.shape[1], -1) @ w_o


def mixer_diff(h_ctx, h_lat, layer_idx, need_ctx, w_qkv, w_o, lq1, lk1, lq2, lk2, subln_g):
    lambda_init = 0.8 - 0.6 * math.exp(-0.3 * layer_idx)
    f32 = jnp.float32
    lam = (jnp.exp(jnp.sum(lq1.astype(f32) * lk1.astype(f32)))
           - jnp.exp(jnp.sum(lq2.astype(f32) * lk2.astype(f32))) + lambda_init)
    scale = A_HEAD_DIM ** -0.5

    def project(h):
        b, s, _ = h.shape
        q, k, v = jnp.split(h @ w_qkv, 3, axis=-1)
        return (q.reshape(b, s, 2 * A_HEADS, A_HEAD_DIM),
                k.reshape(b, s, 2 * A_HEADS, A_HEAD_DIM),
                v.reshape(b, s, A_HEADS, 2 * A_HEAD_DIM))

    def attend(q, k, v):
        s = jnp.einsum('bqhd,bkhd->bhqk', q, k).astype(jnp.float32) * scale
        p = jax.nn.softmax(s, axis=-1)
        p = p.reshape(p.shape[0], A_HEADS, 2, p.shape[2], p.shape[3])
        diff = p[:, :, 0] - lam * p[:, :, 1]
        o = jnp.einsum('bhqk,bkhd->bqhd', diff.astype(v.dtype), v)
        return rms_norm(o, subln_g) * (1.0 - lambda_init)

    qc, kc, vc = project(h_ctx)
    ql, kl, vl = project(h_lat)
    cos, sin = axial_rope_tables(h_lat.shape[1], A_HEAD_DIM)
    ql, kl = apply_rope(ql, cos, sin), apply_rope(kl, cos, sin)
    k_all = jnp.concatenate([kc, kl], axis=1)
    v_all = jnp.concatenate([vc, vl], axis=1)
    ol = sweep_query_blocks(lambda i, qb: attend(qb, k_all, v_all), ql)
    oc = merge_heads(attend(qc, kc, vc), w_o) if need_ctx else None
    return oc, merge_heads(ol, w_o)


def mixer_window(h_ctx, h_lat, need_ctx, w_qkv, w_o, sink):
    nq, nkv = B_HEADS * B_HEAD_DIM, B_KV_HEADS * B_HEAD_DIM
    scale = B_HEAD_DIM ** -0.5

    def project(h):
        b, s, _ = h.shape
        q, k, v = jnp.split(h @ w_qkv, [nq, nq + nkv], axis=-1)
        return (q.reshape(b, s, B_HEADS, B_HEAD_DIM),
                k.reshape(b, s, B_KV_HEADS, B_HEAD_DIM),
                v.reshape(b, s, B_KV_HEADS, B_HEAD_DIM))

    qc, kc, vc = project(h_ctx)
    ql, kl, vl = project(h_lat)
    n_lat = h_lat.shape[1]
    cos, sin = axial_rope_tables(n_lat, B_HEAD_DIM)
    ql, kl = apply_rope(ql, cos, sin), apply_rope(kl, cos, sin)
    pad = ((0, 0), (B_WINDOW, B_WINDOW), (0, 0), (0, 0))
    kl_pad, vl_pad = jnp.pad(kl, pad), jnp.pad(vl, pad)
    span = Q_BLOCK + 2 * B_WINDOW
    ctx_mask = jnp.ones((Q_BLOCK, kc.shape[1]), dtype=bool)

    def block(i, qb):
        start = i * Q_BLOCK
        kw = lax.dynamic_slice_in_dim(kl_pad, start, span, axis=1)
        vw = lax.dynamic_slice_in_dim(vl_pad, start, span, axis=1)
        q_pos = start + jnp.arange(Q_BLOCK)
        k_pos = start - B_WINDOW + jnp.arange(span)
        band = (jnp.abs(q_pos[:, None] - k_pos[None, :]) <= B_WINDOW) & (k_pos >= 0) & (k_pos < n_lat)
        mask = jnp.concatenate([ctx_mask, band], axis=1)
        return gqa_attend(qb, jnp.concatenate([kc, kw], axis=1), jnp.concatenate([vc, vw], axis=1),
                          scale, mask=mask, sink=sink)

    ol = sweep_query_blocks(block, ql)
    oc = merge_heads(gqa_attend(qc, kc, vc, scale, sink=sink), w_o) if need_ctx else None
    return oc, merge_heads(ol, w_o)


def mixer_mla(h_ctx, h_lat, need_ctx, w_down, q_norm_g, kv_norm_g, w_uq, w_ukv, w_o):
    scale = (C_NOPE_DIM + C_ROPE_DIM) ** -0.5

    def project(h, rope):
        b, s, _ = h.shape
        cq, ckv, k_rope = jnp.split(h @ w_down, [C_Q_RANK, C_Q_RANK + C_KV_RANK], axis=-1)
        q = (rms_norm(cq, q_norm_g) @ w_uq).reshape(b, s, C_HEADS, C_NOPE_DIM + C_ROPE_DIM)
        kv = (rms_norm(ckv, kv_norm_g) @ w_ukv).reshape(b, s, C_HEADS, C_NOPE_DIM + C_V_DIM)
        q_nope, q_rope = jnp.split(q, [C_NOPE_DIM], axis=-1)
        k_nope, v = jnp.split(kv, [C_NOPE_DIM], axis=-1)
        k_rope = k_rope[:, :, None, :]
        if rope is not None:
            q_rope = apply_rope(q_rope, *rope)
            k_rope = apply_rope(k_rope, *rope)
        q = jnp.concatenate([q_nope, q_rope], axis=-1)
        k = jnp.concatenate([k_nope, jnp.broadcast_to(k_rope, (b, s, C_HEADS, C_ROPE_DIM))], axis=-1)
        return q, k, v

    qc, kc, vc = project(h_ctx, None)
    ql, kl, vl = project(h_lat, axial_rope_tables(h_lat.shape[1], C_ROPE_DIM))
    k_all = jnp.concatenate([kc, kl], axis=1)
    v_all = jnp.concatenate([vc, vl], axis=1)
    ol = sweep_query_blocks(lambda i, qb: gqa_attend(qb, k_all, v_all, scale), ql)
    oc = merge_heads(gqa_attend(qc, kc, vc, scale), w_o) if need_ctx else None
    return oc, merge_heads(ol, w_o)


def mixer_qknorm(h_ctx, h_lat, need_ctx, w_qkv, q_norm_g, k_norm_g, w_o):
    nq, nkv = D_HEADS * D_HEAD_DIM, D_KV_HEADS * D_HEAD_DIM
    scale = D_HEAD_DIM ** -0.5

    def project(h):
        b, s, _ = h.shape
        q, k, v = jnp.split(h @ w_qkv, [nq, nq + nkv], axis=-1)
        q = rms_norm(q.reshape(b, s, D_HEADS, D_HEAD_DIM), q_norm_g)
        k = rms_norm(k.reshape(b, s, D_KV_HEADS, D_HEAD_DIM), k_norm_g)
        return q, k, v.reshape(b, s, D_KV_HEADS, D_HEAD_DIM)

    qc, kc, vc = project(h_ctx)
    ql, kl, vl = project(h_lat)
    cos, sin = axial_rope_tables(h_lat.shape[1], D_HEAD_DIM)
    ql, kl = apply_rope(ql, cos, sin), apply_rope(kl, cos, sin)
    k_all = jnp.concatenate([kc, kl], axis=1)
    v_all = jnp.concatenate([vc, vl], axis=1)
    ol = sweep_query_blocks(lambda i, qb: gqa_attend(qb, k_all, v_all, scale), ql)
    oc = merge_heads(gqa_attend(qc, kc, vc, scale), w_o) if need_ctx else None
    return oc, merge_heads(ol, w_o)


def conv_ffn(h, w_up, conv_w, conv_b, w_down):
    s = h.shape[1]
    u = h @ w_up
    half = CONV_W // 2
    u_pad = jnp.pad(u, ((0, 0), (half, half), (0, 0)))
    y = conv_b
    for j in range(CONV_W):
        y = y + u_pad[:, j:j + s] * conv_w[j]
    a, g = jnp.split(y, 2, axis=-1)
    return (jax.nn.silu(a) * g) @ w_down


def setup_inputs(seed: int = 0) -> dict:
    key = jax.random.key(seed)
    ks = iter(jax.random.split(key, 40))

    def nrm(shape, scale):
        return jax.random.normal(next(ks), shape, jnp.float32) * scale

    def gain(shape):
        return 1.0 + nrm(shape, 0.02)

    D, L, R = D_MODEL, DEPTH, N_CYCLES
    a_width = 2 * A_HEADS * A_HEAD_DIM
    return {
        'x': nrm((BATCH, SEQ, D), 1.0),
        'c': nrm((BATCH, D), 1.0),
        'ctx': nrm((BATCH, CTX_LEN, D), 1.0),
        'c_ctx': nrm((D,), 1.0),
        'ada_w': nrm((L, D, 6 * D), 0.5 * D ** -0.5),
        'ada_b': nrm((L, 6 * D), 0.01),
        'norm1_g': gain((L, D)),
        'norm2_g': gain((L, D)),
        'ffn_up': nrm((L, D, 2 * D_FF), D ** -0.5),
        'ffn_conv_w': nrm((L, CONV_W, 2 * D_FF), CONV_W ** -0.5),
        'ffn_conv_b': nrm((L, 2 * D_FF), 0.01),
        'ffn_down': nrm((L, D_FF, D), D_FF ** -0.5),
        'a_w_qkv': nrm((R, D, 3 * a_width), D ** -0.5),
        'a_w_o': nrm((R, a_width, D), a_width ** -0.5),
        'a_lambda_q1': nrm((R, A_HEAD_DIM), 0.1),
        'a_lambda_k1': nrm((R, A_HEAD_DIM), 0.1),
        'a_lambda_q2': nrm((R, A_HEAD_DIM), 0.1),
        'a_lambda_k2': nrm((R, A_HEAD_DIM), 0.1),
        'a_subln_g': gain((R, 2 * A_HEAD_DIM)),
        'b_w_qkv': nrm((R, D, (B_HEADS + 2 * B_KV_HEADS) * B_HEAD_DIM), D ** -0.5),
        'b_w_o': nrm((R, B_HEADS * B_HEAD_DIM, D), (B_HEADS * B_HEAD_DIM) ** -0.5),
        'b_sink': nrm((R, B_HEADS), 0.5),
        'c_w_down': nrm((R, D, C_Q_RANK + C_KV_RANK + C_ROPE_DIM), D ** -0.5),
        'c_q_norm_g': gain((R, C_Q_RANK)),
        'c_kv_norm_g': gain((R, C_KV_RANK)),
        'c_w_uq': nrm((R, C_Q_RANK, C_HEADS * (C_NOPE_DIM + C_ROPE_DIM)), C_Q_RANK ** -0.5),
        'c_w_ukv': nrm((R, C_KV_RANK, C_HEADS * (C_NOPE_DIM + C_V_DIM)), C_KV_RANK ** -0.5),
        'c_w_o': nrm((R, C_HEADS * C_V_DIM, D), (C_HEADS * C_V_DIM) ** -0.5),
        'd_w_qkv': nrm((R, D, (D_HEADS + 2 * D_KV_HEADS) * D_HEAD_DIM), D ** -0.5),
        'd_q_norm_g': gain((R, D_HEAD_DIM)),
        'd_k_norm_g': gain((R, D_HEAD_DIM)),
        'd_w_o': nrm((R, D_HEADS * D_HEAD_DIM, D), (D_HEADS * D_HEAD_DIM) ** -0.5),
        'final_g': gain((D,)),
    }


def reference(x, c, ctx, c_ctx, ada_w, ada_b, norm1_g, norm2_g, ffn_up, ffn_conv_w, ffn_conv_b, ffn_down,
              a_w_qkv, a_w_o, a_lambda_q1, a_lambda_k1, a_lambda_q2, a_lambda_k2, a_subln_g,
              b_w_qkv, b_w_o, b_sink,
              c_w_down, c_q_norm_g, c_kv_norm_g, c_w_uq, c_w_ukv, c_w_o,
              d_w_qkv, d_q_norm_g, d_k_norm_g, d_w_o,
              final_g):
    silu_c = jax.nn.silu(c)
    silu_cc = jax.nn.silu(c_ctx)
    for i in range(DEPTH):
        kind, r = i % N_MIXERS, i // N_MIXERS
        need_ctx = i < DEPTH - 1
        mod_l = jnp.split((silu_c @ ada_w[i] + ada_b[i])[:, None, :], 6, axis=-1)
        mod_c = jnp.split(silu_cc @ ada_w[i] + ada_b[i], 6, axis=-1)
        hl = modulate(x, norm1_g[i], mod_l[0], mod_l[1])
        hc = modulate(ctx, norm1_g[i], mod_c[0], mod_c[1])
        if kind == 0:
            oc, ol = mixer_diff(hc, hl, i, need_ctx, a_w_qkv[r], a_w_o[r], a_lambda_q1[r], a_lambda_k1[r],
                                a_lambda_q2[r], a_lambda_k2[r], a_subln_g[r])
        elif kind == 1:
            oc, ol = mixer_window(hc, hl, need_ctx, b_w_qkv[r], b_w_o[r], b_sink[r])
        elif kind == 2:
            oc, ol = mixer_mla(hc, hl, need_ctx, c_w_down[r], c_q_norm_g[r], c_kv_norm_g[r], c_w_uq[r],
                               c_w_ukv[r], c_w_o[r])
        else:
            oc, ol = mixer_qknorm(hc, hl, need_ctx, d_w_qkv[r], d_q_norm_g[r], d_k_norm_g[r], d_w_o[r])
        x = x + mod_l[2] * ol
        x = x + mod_l[5] * conv_ffn(modulate(x, norm2_g[i], mod_l[3], mod_l[4]),
                                    ffn_up[i], ffn_conv_w[i], ffn_conv_b[i], ffn_down[i])
        if need_ctx:
            ctx = ctx + mod_c[2] * oc
            ctx = ctx + mod_c[5] * conv_ffn(modulate(ctx, norm2_g[i], mod_c[3], mod_c[4]),
                                            ffn_up[i], ffn_conv_w[i], ffn_conv_b[i], ffn_down[i])
    return rms_norm(x, final_g)
```

```python
import math
import numpy as np
import ml_dtypes
import concourse.bass as bass
import concourse.mybir as mybir
from concourse.bass_utils import run_bass_kernel_spmd

F32 = mybir.dt.float32
BF16 = mybir.dt.bfloat16
ALU = mybir.AluOpType
AF = mybir.ActivationFunctionType

ENGS = ("pe", "act", "dve", "pool", "sp")

D = 1024
NCTX = 256
NLAT = 2048
T = NCTX + NLAT
DFF = 2816
DEPTH = 4
EPS = 1e-6
TB = [(0, 256), (256, 768), (768, 1280), (1280, 1792), (1792, 2304)]
TILE_W = 2048
NSLOT = 4


class Buf:
    __slots__ = ("t", "name", "last_write", "readers")

    def __init__(self, t, name=""):
        self.t = t
        self.name = name
        self.last_write = None
        self.readers = {}

    def __getitem__(self, idx):
        return self.t[idx]


class Prog:
    def __init__(self, nc):
        self.nc = nc
        self.ops = {e: [] for e in ENGS}
        self.counts = {e: 0 for e in ENGS}
        self.seen = {e: {} for e in ENGS}
        self.sems = {}
        self.dma_counts = {}
        self._stack = []

    def enter(self, cm):
        v = cm.__enter__()
        self._stack.append(cm)
        return v

    def close(self):
        while self._stack:
            self._stack.pop().__exit__(None, None, None)

    def sbuf(self, name, shape, dtype):
        return Buf(self.enter(self.nc.sbuf_tensor(name, list(shape), dtype)), name)

    def psum(self, name, shape, dtype=F32):
        return Buf(self.enter(self.nc.psum_tensor(name, list(shape), dtype)), name)

    def sem(self, key):
        if key not in self.sems:
            self.sems[key] = self.enter(self.nc.semaphore("s_" + str(key)))
        return self.sems[key]

    def _waits(self, eng, reads, writes):
        waits = {}
        seen = self.seen[eng]

        def need(tok):
            if tok is None:
                return
            k, v = tok
            if eng == "pe" and k == "pe":
                return
            if seen.get(k, 0) >= v:
                return
            if waits.get(k, 0) < v:
                waits[k] = v

        for b in reads:
            need(b.last_write)
        for b in writes:
            need(b.last_write)
            for k, v in b.readers.items():
                need((k, v))
        for k, v in waits.items():
            seen[k] = v
        return list(waits.items())

    def op(self, eng, fn, reads=(), writes=()):
        waits = self._waits(eng, reads, writes)
        self.counts[eng] += 1
        n = self.counts[eng]
        self.ops[eng].append((waits, fn, (eng, 1)))
        for b in reads:
            b.readers[eng] = n
        for b in writes:
            b.last_write = (eng, n)
            b.readers = {}

    def dma(self, eng, semkey, fn, reads=(), writes=()):
        waits = self._waits(eng, reads, writes)
        self.sem(semkey)
        self.dma_counts[semkey] = self.dma_counts.get(semkey, 0) + 16
        n = self.dma_counts[semkey]
        self.ops[eng].append((waits, fn, (semkey, 16)))
        for b in reads:
            b.readers[semkey] = n
        for b in writes:
            b.last_write = (semkey, n)
            b.readers = {}

    def wait_all(self, eng, bufs):
        waits = self._waits(eng, bufs, bufs)
        if waits:
            self.ops[eng].append((waits, None, None))

    def emit(self):
        nc = self.nc
        for e in ENGS:
            self.sem(e)
        prog = self

        class _Cnt:
            def __init__(self, e):
                self.e = e
                self.n = 0

            def matmul(self, *a, **k):
                self.n += 1
                return self.e.matmul(*a, **k)

            def transpose(self, *a, **k):
                self.n += 1
                return self.e.transpose(*a, **k)

        def replay(engine, name):
            if name == "pe":
                engine = _Cnt(engine)
                prog.pe_instr_at = []
            for waits, fn, inc in prog.ops[name]:
                if name == "pe":
                    prog.pe_instr_at.append(engine.n)
                    for k, v in waits:
                        engine.e.wait_ge(prog.sems[k], v)
                else:
                    for k, v in waits:
                        engine.wait_ge(prog.sems[k], v)
                if fn is None:
                    continue
                ins = fn(engine)
                ins.then_inc(prog.sems[inc[0]], inc[1])

        with nc.Block() as block:
            @block.tensor
            def _(e):
                replay(e, "pe")

            @block.scalar
            def _(e):
                replay(e, "act")

            @block.vector
            def _(e):
                replay(e, "dve")

            @block.gpsimd
            def _(e):
                replay(e, "pool")

            @block.sync
            def _(e):
                replay(e, "sp")


def fm_unit(wcols):
    k, m = wcols.shape
    n_kc = k // 128
    return np.ascontiguousarray(wcols.reshape(n_kc, 128, m).transpose(1, 0, 2).reshape(128, n_kc * m))


def fmaj(v):
    return np.ascontiguousarray(np.asarray(v, np.float32).reshape(-1, 128).T)


def swap_idx(n, dh):
    p = np.arange(n)
    return (p // dh) * dh + ((p % dh) + dh // 2) % dh


def rope_tables(rot_dim):
    n_rows = NLAT // 64
    rows = np.repeat(np.arange(n_rows, dtype=np.float32), 64)
    cols = np.tile(np.arange(64, dtype=np.float32), n_rows)
    n_freq = rot_dim // 4
    inv_freq = (np.float32(10000.0) ** (-np.arange(n_freq, dtype=np.float32) / np.float32(n_freq))).astype(np.float32)
    ang = np.concatenate([rows[:, None] * inv_freq, cols[:, None] * inv_freq], axis=-1).astype(np.float32)
    cos, sin = np.cos(ang).astype(np.float32), np.sin(ang).astype(np.float32)
    p = np.arange(128)
    d = p % rot_dim
    half = rot_dim // 2
    fi = d % half
    sign = np.where(d < half, -1.0, 1.0).astype(np.float32)
    tab = np.zeros((128, 2, NLAT), np.float32)
    tab[:, 0, :] = cos[:, fi].T
    tab[:, 1, :] = (sin[:, fi] * sign[None, :]).T
    return tab


class WStream:
    def __init__(self):
        self.tiles = []
        self.cur = TILE_W

    def add(self, n, recipe):
        assert n <= TILE_W
        if self.cur + n > TILE_W:
            self.tiles.append([])
            self.cur = 0
        off = self.cur
        self.tiles[-1].append((off, n, recipe))
        self.cur += n
        return len(self.tiles) - 1, off

    def close_tile(self):
        self.cur = TILE_W

    def materialize(self, inp):
        arr = np.zeros((len(self.tiles), 128, TILE_W), np.float32)
        for ti, units in enumerate(self.tiles):
            for off, n, recipe in units:
                a = recipe(inp)
                assert a.shape == (128, n), (a.shape, n)
                arr[ti, :, off:off + n] = a
        return arr


class Gen:
    def __init__(self, nc, n_tiles, nlayers=DEPTH):
        self.nc = nc
        self.P = Prog(nc)
        self.ws = WStream()
        self.n_tiles = n_tiles
        self.nlayers = nlayers
        self.par_cols = 0
        self.dry = False
        self.oldest = 0
        self.par_recipes = []
        self.marks = []
        self.pending_fin = None
        self.const_recipes = {}

    def mark(self, name):
        self.marks.append((name, len(self.P.ops["pe"])))

    def par(self, n, recipe):
        off = self.par_cols
        self.par_cols += n
        self.par_recipes.append((off, n, recipe))
        return off

    def wunit(self, n, recipe, keep=False):
        ti, off = self.ws.add(n, recipe)
        if not keep:
            self.oldest = ti
        assert ti - self.oldest < NSLOT
        self.ensure_tile(ti)
        return self.slots[ti % NSLOT], off

    def ensure_tile(self, ti):
        P = self.P
        hi = min(self.oldest + NSLOT - 1, self.n_tiles - 1)
        assert ti <= hi or self.dry
        while self.next_tile <= hi:
            t = self.next_tile
            slot = self.slots[t % NSLOT]
            if not self.dry:
                src = self.w_d[t]
                P.dma("pool", "w%d" % (t % NSLOT),
                      lambda e, slot=slot, src=src: e.dma_start(out=slot[:, :], in_=src), writes=[slot])
            self.next_tile += 1

    def bank(self, pool):
        lst, idx = self.bpools[pool]
        b = lst[idx[0] % len(lst)]
        idx[0] += 1
        return b

    def pt(self):
        b = self.pts[self.pt_i % len(self.pts)]
        self.pt_i += 1
        return b

    def tmpt(self):
        b = self.tmps[self.tmp_i % len(self.tmps)]
        self.tmp_i += 1
        return b

    def build(self):
        nc, P = self.nc, self.P
        nl = self.nlayers
        self.x_d = nc.dram_tensor("x", [NLAT, D], F32, kind="ExternalInput").ap()
        self.ctx_d = nc.dram_tensor("ctx", [NCTX, D], F32, kind="ExternalInput").ap()
        self.w_d = nc.dram_tensor("wstream", [1 if self.dry else self.n_tiles, 128, TILE_W], F32, kind="ExternalInput").ap()
        self.tab64_d = nc.dram_tensor("tab64", [128, 2, NLAT], F32, kind="ExternalInput").ap()
        self.tab128_d = nc.dram_tensor("tab128", [128, 2, NLAT], F32, kind="ExternalInput").ap()
        self.ident_d = nc.dram_tensor("ident", [128, 128], F32, kind="ExternalInput").ap()
        self.cbf_d = nc.dram_tensor("cbf", [128, 7, 128], BF16, kind="ExternalInput").ap()
        self.out_d = nc.dram_tensor("out", [NLAT, D], F32, kind="ExternalOutput").ap()
        self.next_tile = 0

        self.XT = P.sbuf("XT", [128, 8, T], F32)
        self.XTb = [Buf(self.XT.t, "XT%d" % b) for b in range(5)]
        self.HT = P.sbuf("HT", [128, 8, T], BF16)
        self.HTb = [Buf(self.HT.t, "HT%d" % b) for b in range(5)]
        self.R = P.sbuf("R", [128, 8, T], BF16)
        self.Rs = [Buf(self.R.t, "R%d" % s) for s in range(8)]
        self.OG = P.sbuf("OG", [128, 2, T], BF16)
        self.OGs = [Buf(self.OG.t, "OG%d" % s) for s in range(2)]
        self.TABS = [P.sbuf("TAB%d" % i, [128, 2, 512], F32) for i in range(2)]
        self.tab_i = 0
        self.slots = [P.sbuf("wslot%d" % i, [128, TILE_W], BF16) for i in range(NSLOT)]
        self.pts = [P.sbuf("pt%d" % i, [128, 512], BF16) for i in range(5)]
        self.pt_i = 0
        self.TMP = P.sbuf("TMP", [128, 6, 512], F32)
        self.tmps = [Buf(self.TMP.t, "tmp%d" % i) for i in range(6)]
        self.tmp_i = 0
        self.RST = P.sbuf("RST", [128, 512], F32)
        self.RST2 = P.sbuf("RST2", [128, 512], F32)
        self.epsB = P.sbuf("epsB", [128, 2], F32)
        self.epsb = self.epsB.t
        self.ident = P.sbuf("identS", [128, 128], F32)
        self.cbf = P.sbuf("cbfS", [128, 7, 128], BF16)
        self.MODS = [P.sbuf("MOD%d" % i, [128, 48, 2], F32) for i in range(2)]
        self.AMS = [P.sbuf("AM%d" % i, [128, 2, 8, 2], F32) for i in range(2)]
        self.MOD, self.AM = self.MODS[0], self.AMS[0]
        self.SC = P.sbuf("SCb", [128, 8, 2], BF16)
        self.SM = P.sbuf("SM", [128, 64], F32)
        self.banks = [P.psum("bank%d" % i, [128, 512]) for i in range(8)]
        self.bpools = {
            "g": (self.banks[0:4], [0]),
            "s": (self.banks[0:3], [0]),
            "o": (self.banks[3:7], [0]),
            "f": (self.banks[0:7], [0]),
        }
        self.STAT = self.banks[7]

        self.PARW = 1280
        self.PAR = P.sbuf("PAR", [128, self.PARW], F32)
        self.par_d = nc.dram_tensor("par", [128, self.PARW], F32, kind="ExternalInput").ap()

        P.dma("sp", "c0", lambda e: e.dma_start(out=self.PAR[:, :], in_=self.par_d), writes=[self.PAR])
        P.dma("sp", "c1", lambda e: e.dma_start(out=self.ident[:, :], in_=self.ident_d), writes=[self.ident])
        P.dma("sp", "c2", lambda e: e.dma_start(out=self.cbf[:, :, :], in_=self.cbf_d), writes=[self.cbf])
        self.ones = self.cbf[:, 0, :]
        P.op("dve", lambda e: e.memset(self.epsb[:, :], EPS), writes=[self.epsB])
        o_c = self.par(8, lambda inp, core: fmaj(inp["c"][core]))
        o_cc = self.par(8, lambda inp, core: fmaj(inp["c_ctx"]))
        P.op("act", lambda e: e.activation(out=self.SC[:, :, 0], in_=self.PAR[:, o_c:o_c + 8], func=AF.Silu),
             reads=[self.PAR], writes=[self.SC])
        P.op("act", lambda e: e.activation(out=self.SC[:, :, 1], in_=self.PAR[:, o_cc:o_cc + 8], func=AF.Silu),
             reads=[self.PAR], writes=[self.SC])
        self.mark("load_x")
        self.load_x()
        for li in range(nl):
            self.layer(li)
        self.mark("final")
        self.final()
        self.mark("end")
        P.emit()
        P.close()

    def load_x(self):
        P = self
        Pg = self.P
        stage = self.TMP.t
        for tt in range(18):
            so = 2 * (tt % 2)
            sb = [self.tmps[so], self.tmps[so + 1]]
            if tt < 2:
                src = self.ctx_d[tt * 128:(tt + 1) * 128, :]
            else:
                src = self.x_d[(tt - 2) * 128:(tt - 1) * 128, :]
            b = 0 if tt < 2 else 1 + (tt - 2) // 4
            Pg.dma("sp", "xin%d" % (tt % 2), lambda e, src=src, so=so: e.dma_start(out=stage[:, so:so + 2, :], in_=src.rearrange("p (a f) -> p a f", a=2)),
                   writes=sb)
            for g in range(2):
                bk = self.bank("g")

                def tr(e, g=g, bk=bk, so=so):
                    for i in range(4):
                        c = 4 * g + i
                        ins = e.transpose(out=bk[:, i * 128:(i + 1) * 128], in_=stage[:, so + c // 4, (c % 4) * 128:(c % 4 + 1) * 128],
                                          identity=self.ident[:, :])
                    return ins
                Pg.op("pe", tr, reads=sb + [self.ident], writes=[bk])
                eng = "act" if g == 0 else "dve"
                dst = self.XT[:, 4 * g:4 * g + 4, tt * 128:(tt + 1) * 128]
                srcp = bk[:, :].rearrange("p (a f) -> p a f", a=4)
                if eng == "act":
                    Pg.op("act", lambda e, dst=dst, srcp=srcp: e.activation(out=dst, in_=srcp, func=AF.Copy),
                          reads=[bk], writes=[self.XTb[b]])
                else:
                    Pg.op("dve", lambda e, dst=dst, srcp=srcp: e.tensor_copy(out=dst, in_=srcp),
                          reads=[bk], writes=[self.XTb[b]])
            self.ada_tile(0, tt)

    def rms_stat(self, srcs, width, n_feat, reads):
        Pg = self.P
        n = len(srcs)
        for i, s in enumerate(srcs):
            sq = self.pt()
            Pg.op("act", lambda e, s=s, sq=sq: e.activation(out=sq[:, 0:width], in_=s, func=AF.Square),
                  reads=reads, writes=[sq])
            Pg.op("pe", lambda e, sq=sq, i=i: e.matmul(self.STAT[:, 0:width], lhsT=self.ones, rhs=sq[:, 0:width],
                                                     start=(i == 0), stop=(i == n - 1)),
                  reads=[sq, self.cbf], writes=[self.STAT])
        r = self.RST
        Pg.op("act", lambda e: e.activation(out=r[:, 0:width], in_=self.STAT[:, 0:width], func=AF.Ln,
                                            bias=self.epsb[:, 0:1], scale=1.0 / n_feat),
              reads=[self.STAT, self.epsB], writes=[r])
        Pg.op("act", lambda e: e.activation(out=r[:, 0:width], in_=r[:, 0:width], func=AF.Exp, scale=-0.5),
              reads=[r], writes=[r])
        return r

    def tap(self, r):
        if r is self.RST:
            return self.RST[:, :]
        return self.TMP.t[:, self.tmps.index(r), :]

    def modulate(self, which, blocks):
        Pg = self.P
        blocks = list(blocks)
        rt = [self.RST, self.RST2]

        def stage1(i):
            b = blocks[i]
            t0, t1 = TB[b]
            w = t1 - t0
            for c in range(8):
                sq = self.pt()
                xin = self.XT[:, c, t0:t1]
                if c % 2 == 0:
                    Pg.op("act", lambda e, sq=sq, xin=xin, w=w: e.activation(out=sq[:, 0:w], in_=xin, func=AF.Square),
                          reads=[self.XTb[b]], writes=[sq])
                else:
                    Pg.op("pool", lambda e, sq=sq, xin=xin, w=w: e.tensor_tensor(out=sq[:, 0:w], in0=xin, in1=xin, op=ALU.mult),
                          reads=[self.XTb[b]], writes=[sq])
                Pg.op("pe", lambda e, sq=sq, c=c, w=w: e.matmul(self.STAT[:, 0:w], lhsT=self.ones, rhs=sq[:, 0:w],
                                                              start=(c == 0), stop=(c == 7)),
                      reads=[sq, self.cbf], writes=[self.STAT])
            r = rt[i % 2]
            Pg.op("act", lambda e, r=r, w=w: e.activation(out=r[:, 0:w], in_=self.STAT[:, 0:w], func=AF.Sqrt,
                                                        bias=self.epsb[:, 0:1], scale=1.0 / D),
                  reads=[self.STAT, self.epsB], writes=[r])
            Pg.op("dve", lambda e, r=r, w=w: e.reciprocal(out=r[:, 0:w], in_=r[:, 0:w]), reads=[r], writes=[r])
            return r

        def stage2(i, r):
            b = blocks[i]
            t0, t1 = TB[b]
            w = t1 - t0
            j = 1 if b == 0 else 0
            rap = r[:, 0:w]
            for c in range(8):
                tm = self.tmpt()
                tma = self.tap(tm)[:, 0:w]
                am = self.AM[:, which, c, j:j + 1]
                sh = self.MOD[:, (3 * which) * 8 + c, j:j + 1]
                xin = self.XT[:, c, t0:t1]
                hout = self.HT[:, c, t0:t1]
                Pg.op("dve", lambda e, tma=tma, xin=xin, am=am, rap=rap: e.scalar_tensor_tensor(
                    out=tma, in0=xin, scalar=am, in1=rap, op0=ALU.mult, op1=ALU.mult),
                    reads=[self.XTb[b], self.AM, r], writes=[tm])
                Pg.op("act", lambda e, tma=tma, hout=hout, sh=sh: e.activation(
                    out=hout, in_=tma, func=AF.Identity, bias=sh, scale=1.0),
                    reads=[tm, self.MOD], writes=[self.HTb[b]])

        rs = {0: stage1(0)}
        for i in range(len(blocks)):
            if i + 1 < len(blocks):
                rs[i + 1] = stage1(i + 1)
            stage2(i, rs[i])

    def ada_tile(self, li, m2):
        Pg = self.P
        bk = self.banks[7]
        self.ws.close_tile()
        units = []
        for m in (2 * m2, 2 * m2 + 1):
            slot, off = self.wunit(1024, lambda inp, m=m: fm_unit(inp["ada_w"][li][:, m * 128:(m + 1) * 128]), keep=(m % 2 == 1))
            units.append((m, slot, off))

        def fn(e, units=units):
            for m, slot, off in units:
                for kc in range(8):
                    ins = e.matmul(bk[:, 2 * m:2 * m + 2], lhsT=slot[:, off + kc * 128: off + (kc + 1) * 128],
                                   rhs=self.SC[:, kc, :], start=(kc == 0), stop=(kc == 7))
            return ins
        Pg.op("pe", fn, reads=[units[0][1], self.SC], writes=[bk])

    def ada_finish(self, li):
        Pg = self.P
        ob = self.par(48, lambda inp, core: fmaj(inp["ada_b"][li]))
        og1 = self.par(8, lambda inp, core: fmaj(inp["norm1_g"][li]))
        og2 = self.par(8, lambda inp, core: fmaj(inp["norm2_g"][li]))
        bk = self.banks[7]
        MOD = self.MODS[li % 2]
        AM = self.AMS[li % 2]
        self.MOD, self.AM = MOD, AM
        for j in range(2):
            Pg.op("dve", lambda e, j=j: e.tensor_tensor(out=MOD[:, :, j], in0=bk[:, 0:96].rearrange("p (m j) -> p m j", j=2)[:, :, j],
                                                        in1=self.PAR[:, ob:ob + 48], op=ALU.add),
                  reads=[bk, self.PAR], writes=[MOD])
        for which, og in ((0, og1), (1, og2)):
            for j in range(2):
                sc = MOD[:, (3 * which + 1) * 8:(3 * which + 2) * 8, j]
                Pg.op("dve", lambda e, which=which, j=j, sc=sc, og=og: e.scalar_tensor_tensor(
                    out=AM[:, which, :, j], in0=sc, scalar=1.0, in1=self.PAR[:, og:og + 8], op0=ALU.add, op1=ALU.mult),
                    reads=[MOD, self.PAR], writes=[AM])

    def load_tab(self, tab_d, b):
        t0, t1 = TB[b]
        self.tab_i += 1
        tab = self.TABS[self.tab_i % 2]
        self.TAB = tab
        self.P.dma("sp", "tab%d" % (self.tab_i % 2), lambda e: e.dma_start(out=tab[:, :, :], in_=tab_d[:, :, t0 - NCTX:t1 - NCTX]),
                   writes=[tab])
        return tab

    def proj_fm(self, dst, src_chunks, src_bufs, n_kc, rec, rec_swap, mode, tab_d=None, blocks=range(5),
                gpar=None, gspar=None, dst2=None):
        Pg = self.P
        slot, off = self.wunit(n_kc * 128, rec)
        perm = self.cbf[:, 6 if mode == "normrope" else 5, :]
        si = self.Rs.index(dst)
        si2 = self.Rs.index(dst2) if dst2 is not None else None
        def body(b):
            t0, t1 = TB[b]
            w = t1 - t0
            rhs = src_chunks(t0, t1)
            bk = self.bank("g")

            def fn(e, slot=slot, off=off, rhs=rhs, bk=bk, w=w):
                for kc in range(n_kc):
                    ins = e.matmul(bk[:, 0:w], lhsT=slot[:, off + kc * 128: off + (kc + 1) * 128], rhs=rhs[kc],
                                   start=(kc == 0), stop=(kc == n_kc - 1))
                return ins
            Pg.op("pe", fn, reads=[slot, src_bufs[b]], writes=[bk])
            dsta = self.R[:, si, t0:t1]
            roped = (mode in ("rope", "normrope")) and b > 0
            if roped:
                TAB = self.load_tab(tab_d, b)
            yield
            if roped:
                qb = self.pt()
                Pg.op("dve", lambda e: e.tensor_copy(out=qb[:, 0:w], in_=bk[:, 0:w]), reads=[bk], writes=[qb])
                bk2 = self.bank("g")
                Pg.op("pe", lambda e: e.matmul(bk2[:, 0:w], lhsT=perm, rhs=qb[:, 0:w], start=True, stop=True),
                      reads=[qb, self.cbf], writes=[bk2])
            if dst2 is not None:
                assert mode == "rope"
                dlo = self.R[0:64, si, t0:t1]
                dhi = self.R[64:128, si2, t0:t1]
                if b == 0:
                    Pg.op("act", lambda e, dlo=dlo, bk=bk, w=w: e.activation(out=dlo, in_=bk[0:64, 0:w], func=AF.Copy),
                          reads=[bk], writes=[dst])
                    Pg.op("act", lambda e, dhi=dhi, bk=bk, w=w: e.activation(out=dhi, in_=bk[64:128, 0:w], func=AF.Copy),
                          reads=[bk], writes=[dst2])
                else:
                    ta, tb_ = self.tmpt(), self.tmpt()
                    Pg.op("dve", lambda e, TAB=TAB, ta=ta, bk=bk, w=w: e.tensor_tensor(out=self.tap(ta)[:, 0:w], in0=bk[:, 0:w], in1=TAB[:, 0, 0:w], op=ALU.mult),
                          reads=[bk, TAB], writes=[ta])
                    Pg.op("dve", lambda e, TAB=TAB, tb_=tb_, bk2=bk2, w=w: e.tensor_tensor(out=self.tap(tb_)[:, 0:w], in0=bk2[:, 0:w], in1=TAB[:, 1, 0:w], op=ALU.mult),
                          reads=[bk2, TAB], writes=[tb_])
                    Pg.op("dve", lambda e, ta=ta, tb_=tb_, dlo=dlo, w=w: e.tensor_tensor(out=dlo, in0=self.tap(ta)[0:64, 0:w], in1=self.tap(tb_)[0:64, 0:w], op=ALU.add),
                          reads=[ta, tb_], writes=[dst])
                    Pg.op("pool", lambda e, ta=ta, tb_=tb_, dhi=dhi, w=w: e.tensor_tensor(out=dhi, in0=self.tap(ta)[64:128, 0:w], in1=self.tap(tb_)[64:128, 0:w], op=ALU.add),
                          reads=[ta, tb_], writes=[dst2])
            elif mode == "copy" or (mode == "rope" and b == 0):
                Pg.op("act", lambda e, dsta=dsta, bk=bk, w=w: e.activation(out=dsta, in_=bk[:, 0:w], func=AF.Copy),
                      reads=[bk], writes=[dst])
            elif mode == "rope":
                ta, tb_ = self.tmpt(), self.tmpt()
                Pg.op("dve", lambda e, TAB=TAB, ta=ta, bk=bk, w=w: e.tensor_tensor(out=self.tap(ta)[:, 0:w], in0=bk[:, 0:w], in1=TAB[:, 0, 0:w], op=ALU.mult),
                      reads=[bk, TAB], writes=[ta])
                Pg.op("dve", lambda e, TAB=TAB, tb_=tb_, bk2=bk2, w=w: e.tensor_tensor(out=self.tap(tb_)[:, 0:w], in0=bk2[:, 0:w], in1=TAB[:, 1, 0:w], op=ALU.mult),
                      reads=[bk2, TAB], writes=[tb_])
                Pg.op("dve", lambda e, ta=ta, tb_=tb_, dsta=dsta, w=w: e.tensor_tensor(out=dsta, in0=self.tap(ta)[:, 0:w], in1=self.tap(tb_)[:, 0:w], op=ALU.add),
                      reads=[ta, tb_], writes=[dst])
            elif mode == "normrope":
                r = self.rms_stat([bk[:, 0:w]], w, 128, [bk])
                rap = self.tap(r)[:, 0:w]
                g = self.PAR[:, gpar:gpar + 1]
                if b == 0:
                    Pg.op("dve", lambda e, dsta=dsta, bk=bk, w=w, g=g, rap=rap: e.scalar_tensor_tensor(
                        out=dsta, in0=bk[:, 0:w], scalar=g, in1=rap, op0=ALU.mult, op1=ALU.mult),
                        reads=[bk, self.PAR, r], writes=[dst])
                else:
                    gs = self.PAR[:, gspar:gspar + 1]
                    ta, tb_ = self.tmpt(), self.tmpt()
                    Pg.op("dve", lambda e, ta=ta, bk=bk, w=w, g=g, TAB=TAB: e.scalar_tensor_tensor(
                        out=self.tap(ta)[:, 0:w], in0=bk[:, 0:w], scalar=g, in1=TAB[:, 0, 0:w], op0=ALU.mult, op1=ALU.mult),
                        reads=[bk, TAB, self.PAR], writes=[ta])
                    Pg.op("dve", lambda e, tb_=tb_, bk2=bk2, w=w, gs=gs, TAB=TAB: e.scalar_tensor_tensor(
                        out=self.tap(tb_)[:, 0:w], in0=bk2[:, 0:w], scalar=gs, in1=TAB[:, 1, 0:w], op0=ALU.mult, op1=ALU.mult),
                        reads=[bk2, TAB, self.PAR], writes=[tb_])
                    Pg.op("dve", lambda e, ta=ta, tb_=tb_, w=w: e.tensor_tensor(out=self.tap(ta)[:, 0:w], in0=self.tap(ta)[:, 0:w], in1=self.tap(tb_)[:, 0:w], op=ALU.add),
                          reads=[ta, tb_], writes=[ta])
                    Pg.op("dve", lambda e, ta=ta, dsta=dsta, w=w, rap=rap: e.tensor_tensor(out=dsta, in0=self.tap(ta)[:, 0:w], in1=rap, op=ALU.mult),
                          reads=[ta, r], writes=[dst])

        prev = None
        for b in blocks:
            g_ = body(b)
            next(g_)
            if prev is not None:
                for _ in prev:
                    pass
            prev = g_
        if prev is not None:
            for _ in prev:
                pass

    def proj_tm(self, dst, src_tile, src_bufs, n_kc, rec):
        Pg = self.P
        slot, off = self.wunit(n_kc * 128, rec)
        si = self.Rs.index(dst)
        for q in range(5):
            tts = list(range(4 * q, min(4 * q + 4, 18)))
            bk = self.bank("g")
            bset = sorted(set((0 if tt < 2 else 1 + (tt - 2) // 4) for tt in tts))

            def fn(e, tts=tts, bk=bk, slot=slot, off=off):
                for i, tt in enumerate(tts):
                    for kc in range(n_kc):
                        ins = e.matmul(bk[:, i * 128:(i + 1) * 128], lhsT=src_tile(kc, tt),
                                       rhs=slot[:, off + kc * 128: off + (kc + 1) * 128], start=(kc == 0), stop=(kc == n_kc - 1))
                return ins
            Pg.op("pe", fn, reads=[slot] + [src_bufs[b] for b in bset], writes=[bk])
            n = len(tts)
            dsta = self.R[:, si, tts[0] * 128:(tts[0] + n) * 128]
            Pg.op("act", lambda e, dsta=dsta, bk=bk, n=n: e.activation(out=dsta, in_=bk[:, 0:n * 128], func=AF.Copy),
                  reads=[bk], writes=[dst])

    def attention(self, units, scale, qblocks, window=False, zadd=None):
        Pg = self.P
        LOOK = 2
        for b in qblocks:
            t0, t1 = TB[b]
            w = t1 - t0
            if b == 0:
                ktl = [(0, 0, w, []), (1, 0, w, [])]
            elif not window:
                ktl = [(kt, 0, w, []) for kt in range(18)]
            else:
                ktl = [(0, 0, w, []), (1, 0, w, [])]
                i0 = (t0 - NCTX) // 128
                for jk in range(max(i0 - 1, 0), min(i0 + 5, 16)):
                    lo = max(jk - 1, i0)
                    hi = min(jk + 1, i0 + 3)
                    masks = []
                    if lo <= jk + 1 <= hi:
                        masks.append((3, (jk + 1 - i0) * 128, (jk + 2 - i0) * 128))
                    if lo <= jk - 1 <= hi:
                        masks.append((4, (jk - 1 - i0) * 128, (jk - i0) * 128))
                    ktl.append((2 + jk, (lo - i0) * 128, (hi + 1 - i0) * 128, masks))
            for u in units:
                diff = u["mode"] == "diff"
                passes = [[m] for m in u["maps"]] if diff else [u["maps"]]
                an = []
                for pi_, maps in enumerate(passes):
                    O = self.bank("o")
                    Z = self.bank("o")

                    def qk(ki, mi, maps=maps):
                        kt, c0, c1, _ = ktl[ki]
                        bk = self.bank("s")
                        terms = maps[mi]["terms"]

                        def fn(e, bk=bk, terms=terms, kt=kt, c0=c0, c1=c1, t0=t0):
                            for i, (kbuf, ks, qbuf, qs) in enumerate(terms):
                                ins = e.matmul(bk[:, c0:c1], lhsT=self.R[:, ks, kt * 128:(kt + 1) * 128],
                                               rhs=self.R[:, qs, t0 + c0:t0 + c1], start=(i == 0), stop=(i == len(terms) - 1))
                            return ins
                        rd = []
                        for (kbuf, ks, qbuf, qs) in terms:
                            rd += [kbuf, qbuf]
                        Pg.op("pe", fn, reads=rd, writes=[bk])
                        return bk

                    seq = [(ki, mi) for ki in range(len(ktl)) for mi in range(len(maps))]
                    sb = {}
                    nxt = 0
                    for si_, (ki, mi) in enumerate(seq):
                        while nxt < len(seq) and nxt <= si_ + LOOK:
                            sb[seq[nxt]] = qk(*seq[nxt])
                            nxt += 1
                        kt, c0, c1, masks = ktl[ki]
                        m = maps[mi]
                        bk = sb.pop((ki, mi))
                        pt = self.pt()
                        Pg.op("act", lambda e, pt=pt, bk=bk, c0=c0, c1=c1: e.activation(out=pt[:, c0:c1], in_=bk[:, c0:c1], func=AF.Exp, scale=scale),
                              reads=[bk], writes=[pt])
                        for (mk, a0, a1) in masks:
                            Pg.op("pool", lambda e, pt=pt, mk=mk, a0=a0, a1=a1: e.tensor_tensor(out=pt[:, a0:a1], in0=pt[:, a0:a1], in1=self.cbf[:, mk, :], op=ALU.mult),
                                  reads=[pt, self.cbf], writes=[pt])
                        vbuf, vs = m["v"]
                        onesap = m["ones"]
                        st = (si_ == 0)
                        last = (si_ == len(seq) - 1)

                        def pv(e, pt=pt, c0=c0, c1=c1, vs=vs, kt=kt, onesap=onesap, st=st, last=last, O=O, Z=Z):
                            e.matmul(O[:, c0:c1], lhsT=self.R[:, vs, kt * 128:(kt + 1) * 128], rhs=pt[:, c0:c1], start=st, stop=last)
                            return e.matmul(Z[:, c0:c1], lhsT=onesap, rhs=pt[:, c0:c1], start=st, stop=last)
                        Pg.op("pe", pv, reads=[pt, vbuf, self.cbf], writes=[O, Z])
                    rz = self.tmpt()
                    rza = self.tap(rz)[:, 0:w]
                    if zadd is not None:
                        za = zadd(u)
                        Pg.op("dve", lambda e, rza=rza, Z=Z, za=za, w=w: e.tensor_scalar(out=rza, in0=Z[:, 0:w], scalar1=za, scalar2=None, op0=ALU.add),
                              reads=[Z, self.SM], writes=[rz])
                        Pg.op("dve", lambda e, rza=rza: e.reciprocal(out=rza, in_=rza), reads=[rz], writes=[rz])
                    else:
                        Pg.op("dve", lambda e, rza=rza, Z=Z, w=w: e.reciprocal(out=rza, in_=Z[:, 0:w]), reads=[Z], writes=[rz])
                    og = u["og"]
                    dst = self.OG[:, og, t0:t1]
                    if not diff:
                        Pg.op("dve", lambda e, rza=rza, O=O, dst=dst, w=w: e.tensor_tensor(out=dst, in0=O[:, 0:w], in1=rza, op=ALU.mult),
                              reads=[O, rz], writes=[self.OGs[og]])
                    else:
                        Pg.op("dve", lambda e, rza=rza, O=O, w=w: e.tensor_tensor(out=rza, in0=O[:, 0:w], in1=rza, op=ALU.mult),
                              reads=[O, rz], writes=[rz])
                        an.append(rz)
                    if pi_ == 0 and self.pending_fin is not None:
                        self.pending_fin()
                        self.pending_fin = None
                if diff:
                    a0a_, a1a_ = self.tap(an[0])[:, 0:w], self.tap(an[1])[:, 0:w]
                    nl0_ = self.SM[:, 8:9]
                    Pg.op("dve", lambda e, a0a_=a0a_, a1a_=a1a_, nl0_=nl0_: e.scalar_tensor_tensor(out=a0a_, in0=a1a_, scalar=nl0_, in1=a0a_, op0=ALU.mult, op1=ALU.add),
                          reads=[an[0], an[1], self.SM], writes=[an[0]])

                    def fin(an=an, w=w, dst=dst, og=og):
                        a0a = self.tap(an[0])[:, 0:w]
                        r = self.rms_stat([a0a], w, 128, [an[0]])
                        rap = self.tap(r)[:, 0:w]
                        sg = self.SM[:, 9:10]
                        Pg.op("dve", lambda e: e.scalar_tensor_tensor(out=dst, in0=a0a, scalar=sg, in1=rap, op0=ALU.mult, op1=ALU.mult),
                              reads=[an[0], self.SM, r], writes=[self.OGs[og]])
                    self.pending_fin = fin
        if self.pending_fin is not None:
            self.pending_fin()
            self.pending_fin = None

    def wo_apply(self, li, n_u, rec, blocks):
        Pg = self.P
        slot, off = self.wunit(n_u * 1024, rec)
        for b in blocks:
            t0, t1 = TB[b]
            w = t1 - t0
            j = 1 if b == 0 else 0
            for c in range(8):
                bk = self.bank("g")

                def fn(e, bk=bk, c=c, w=w, t0=t0, t1=t1):
                    for u in range(n_u):
                        ins = e.matmul(bk[:, 0:w], lhsT=slot[:, off + u * 1024 + c * 128: off + u * 1024 + (c + 1) * 128],
                                       rhs=self.OG[:, u, t0:t1], start=(u == 0), stop=(u == n_u - 1))
                    return ins
                Pg.op("pe", fn, reads=[slot] + self.OGs[0:n_u], writes=[bk])
                gate = self.MOD[:, 2 * 8 + c, j:j + 1]
                Pg.op("dve", lambda e, bk=bk, c=c, w=w, gate=gate, t0=t0, t1=t1: e.scalar_tensor_tensor(
                    out=self.XT[:, c, t0:t1], in0=bk[:, 0:w], scalar=gate, in1=self.XT[:, c, t0:t1], op0=ALU.mult, op1=ALU.add),
                    reads=[bk, self.MOD, self.XTb[b]], writes=[self.XTb[b]])

    def ffn(self, li, blocks):
        Pg = self.P
        ocw = self.par(132, lambda inp, core: np.ascontiguousarray(
            inp["ffn_conv_w"][li].reshape(3, 44, 128).transpose(2, 0, 1).reshape(128, 132)))
        ocb = self.par(44, lambda inp, core: fmaj(inp["ffn_conv_b"][li]))
        parts = [(0, 6), (6, 12), (12, 18), (18, 22)]
        self.ypair = 0

        def up(jj, meng="pool"):
            s = jj % 8
            self.ws.close_tile()
            ua = self.wunit(1024, lambda inp, jj=jj: fm_unit(inp["ffn_up"][li][:, jj * 128:(jj + 1) * 128]))
            ug = self.wunit(1024, lambda inp, jj=jj: fm_unit(inp["ffn_up"][li][:, DFF + jj * 128: DFF + (jj + 1) * 128]), keep=True)
            prev = None
            for b in blocks:
                t0, t1 = TB[b]
                w = t1 - t0
                bks, ys = [], []
                for hi, (slot, off) in enumerate((ua, ug)):
                    bk = self.bank("f")

                    def fn(e, bk=bk, slot=slot, off=off, w=w, t0=t0, t1=t1):
                        for kc in range(8):
                            ins = e.matmul(bk[:, 0:w], lhsT=slot[:, off + kc * 128: off + (kc + 1) * 128],
                                           rhs=self.HT[:, kc, t0:t1], start=(kc == 0), stop=(kc == 7))
                        return ins
                    Pg.op("pe", fn, reads=[slot, self.HTb[b]], writes=[bk])
                    bks.append(bk)
                yp = self.ypair % 3
                self.ypair += 1
                cw = []
                for hi in range(2):
                    bk = bks[hi]
                    ch = jj + 22 * hi
                    y = self.tmps[2 * yp + hi]
                    ya = self.tap(y)
                    w0 = self.PAR[:, ocw + 0 * 44 + ch: ocw + 0 * 44 + ch + 1]
                    w1 = self.PAR[:, ocw + 1 * 44 + ch: ocw + 1 * 44 + ch + 1]
                    w2 = self.PAR[:, ocw + 2 * 44 + ch: ocw + 2 * 44 + ch + 1]
                    cb = self.PAR[:, ocb + ch: ocb + ch + 1]
                    cw.append((bk, y, ya, w0, w2))
                    ys.append(y)
                    Pg.op("act", lambda e, ya=ya, bk=bk, w=w, w1=w1, cb=cb: e.activation(out=ya[:, 0:w], in_=bk[:, 0:w], func=AF.Identity, bias=cb, scale=w1),
                          reads=[bk, self.PAR], writes=[y])
                for (bk, y, ya, w0, w2) in cw:
                    Pg.op("dve", lambda e, ya=ya, bk=bk, w=w, w0=w0: e.scalar_tensor_tensor(
                        out=ya[:, 1:w], in0=bk[:, 0:w - 1], scalar=w0, in1=ya[:, 1:w], op0=ALU.mult, op1=ALU.add),
                        reads=[bk, self.PAR, y], writes=[y])
                for (bk, y, ya, w0, w2) in cw:
                    Pg.op("dve", lambda e, ya=ya, bk=bk, w=w, w2=w2: e.scalar_tensor_tensor(
                        out=ya[:, 0:w - 1], in0=bk[:, 1:w], scalar=w2, in1=ya[:, 0:w - 1], op0=ALU.mult, op1=ALU.add),
                        reads=[bk, self.PAR, y], writes=[y])
                if prev is not None and prev[0] >= 1 and b >= 2:
                    for hi, (bk, y, ya, w0, w2) in enumerate(cw):
                        pbk = prev[1][hi]
                        Pg.op("dve", lambda e, ya=ya, pbk=pbk, w0=w0: e.scalar_tensor_tensor(
                            out=ya[:, 0:1], in0=pbk[:, 511:512], scalar=w0, in1=ya[:, 0:1], op0=ALU.mult, op1=ALU.add),
                            reads=[pbk, self.PAR, y], writes=[y])
                    for hi, (bk, y, ya, w0, w2) in enumerate(cw):
                        py = prev[2][hi]
                        pya = self.tap(py)
                        Pg.op("dve", lambda e, pya=pya, bk=bk, w2=w2: e.scalar_tensor_tensor(
                            out=pya[:, 511:512], in0=bk[:, 0:1], scalar=w2, in1=pya[:, 511:512], op0=ALU.mult, op1=ALU.add),
                            reads=[bk, self.PAR, py], writes=[py])
                if prev is not None:
                    self.ffn_fin(prev, s, meng)
                prev = (b, bks, ys)
            self.ffn_fin(prev, s, meng)
            if li + 1 < self.nlayers:
                self.ada_tile(li + 1, jj)

        def down(j0, j1):
            nch = j1 - j0
            wd = []
            self.ws.close_tile()
            for s_ in range(nch):
                wd.append(self.wunit(1024, lambda inp, jj=j0 + s_: np.ascontiguousarray(inp["ffn_down"][li][jj * 128:(jj + 1) * 128, :]), keep=(s_ > 0)))
            wslots = []
            for (sl, _) in wd:
                if sl not in wslots:
                    wslots.append(sl)
            mslots = [(j0 + s_) % 8 for s_ in range(nch)]
            for b in blocks:
                t0, t1 = TB[b]
                w = t1 - t0
                j = 1 if b == 0 else 0
                for c in range(8):
                    bk = self.bank("g")

                    def fn(e, bk=bk, c=c, w=w, t0=t0, t1=t1, wd=wd, nch=nch, mslots=mslots):
                        for s_ in range(nch):
                            slot, off = wd[s_]
                            ins = e.matmul(bk[:, 0:w], lhsT=slot[:, off + c * 128: off + (c + 1) * 128],
                                           rhs=self.R[:, mslots[s_], t0:t1], start=(s_ == 0), stop=(s_ == nch - 1))
                        return ins
                    Pg.op("pe", fn, reads=wslots + [self.Rs[m] for m in mslots], writes=[bk])
                    gate = self.MOD[:, 5 * 8 + c, j:j + 1]
                    Pg.op("dve", lambda e, bk=bk, c=c, w=w, gate=gate, t0=t0, t1=t1: e.scalar_tensor_tensor(
                        out=self.XT[:, c, t0:t1], in0=bk[:, 0:w], scalar=gate, in1=self.XT[:, c, t0:t1], op0=ALU.mult, op1=ALU.add),
                        reads=[bk, self.MOD, self.XTb[b]], writes=[self.XTb[b]])

        tasks = []
        for pi, (j0, j1) in enumerate(parts):
            for jj in range(j0, j1):
                tasks.append(("up", jj))
                if pi > 0 and jj == j0 + 1:
                    tasks.append(("down", parts[pi - 1]))
        tasks.append(("down", parts[-1]))
        hoisted = set()
        for pi, (j0, j1) in enumerate(parts):
            if pi > 0:
                hoisted.update((j0, j0 + 1))
        hoisted.update((parts[-1][1] - 2, parts[-1][1] - 1))
        for t in tasks:
            if t[0] == "up":
                up(t[1], "dve" if t[1] in hoisted else "pool")
            else:
                down(*t[1])

    def ffn_fin(self, prev, s, eng="pool"):
        Pg = self.P
        b, bks, ys = prev
        t0, t1 = TB[b]
        w = t1 - t0
        ya, yg = self.tap(ys[0]), self.tap(ys[1])
        dst = self.R[:, s, t0:t1]
        Pg.op("act", lambda e: e.activation(out=ya[:, 0:w], in_=ya[:, 0:w], func=AF.Silu), reads=[ys[0]], writes=[ys[0]])
        Pg.op(eng, lambda e: e.tensor_tensor(out=dst, in0=ya[:, 0:w], in1=yg[:, 0:w], op=ALU.mult),
              reads=[ys[0], ys[1]], writes=[self.Rs[s]])

    def pipeline(self, n, proj, attn, wo):
        proj(0)
        for h in range(n):
            attn(h)
            if h + 1 < n:
                proj(h + 1)
            wo(h)

    def layer(self, li):
        kind = li % 4
        need_ctx = li < DEPTH - 1
        self.mark("L%d ada" % li)
        if li == 0:
            for m2 in range(18, 24):
                self.ada_tile(li, m2)
        self.ada_finish(li)
        self.mark("L%d mod1" % li)
        self.modulate(0, range(5))
        self.mark("L%d mixer" % li)
        qblocks = list(range(5)) if need_ctx else list(range(1, 5))
        Rs = self.Rs
        Pg = self.P
        ht_src = lambda t0, t1: [self.HT[:, kc, t0:t1] for kc in range(8)]
        ht_tile = lambda kc, tt: self.HT[:, kc, tt * 128:(tt + 1) * 128]
        full_ones = self.cbf[:, 0, :]

        if kind == 0:
            W = lambda inp: inp["a_w_qkv"][0]
            sw = swap_idx(128, 64)
            ol = self.par(256, lambda inp, core: np.ascontiguousarray(np.broadcast_to(np.concatenate(
                [inp["a_lambda_q1"][0], inp["a_lambda_k1"][0], inp["a_lambda_q2"][0], inp["a_lambda_k2"][0]])[None, :], (128, 256))))
            osg = self.par(1, lambda inp, core: np.ascontiguousarray(inp["a_subln_g"][0].reshape(128, 1)))
            lam_init = 0.8 - 0.6 * math.exp(-0.3 * li)
            SM = self.SM
            tm = self.tmpt()
            tma = self.tap(tm)
            Pg.op("dve", lambda e: e.tensor_tensor(out=tma[:, 0:64], in0=self.PAR[:, ol:ol + 64], in1=self.PAR[:, ol + 64:ol + 128], op=ALU.mult),
                  reads=[self.PAR], writes=[tm])
            Pg.op("dve", lambda e: e.tensor_tensor(out=tma[:, 64:128], in0=self.PAR[:, ol + 128:ol + 192], in1=self.PAR[:, ol + 192:ol + 256], op=ALU.mult),
                  reads=[self.PAR, tm], writes=[tm])
            Pg.op("dve", lambda e: e.tensor_reduce(out=SM[:, 0:2], in_=tma[:, 0:128].rearrange("p (a f) -> p a f", a=2), axis=mybir.AxisListType.X, op=ALU.add),
                  reads=[tm], writes=[SM])
            Pg.op("act", lambda e: e.activation(out=SM[:, 2:4], in_=SM[:, 0:2], func=AF.Exp), reads=[SM], writes=[SM])
            Pg.op("dve", lambda e: e.tensor_tensor(out=SM[:, 4:5], in0=SM[:, 3:4], in1=SM[:, 2:3], op=ALU.subtract), reads=[SM], writes=[SM])
            Pg.op("dve", lambda e: e.tensor_scalar(out=SM[:, 8:9], in0=SM[:, 4:5], scalar1=-lam_init, scalar2=None, op0=ALU.add), reads=[SM], writes=[SM])
            Pg.op("dve", lambda e: e.tensor_scalar(out=SM[:, 9:10], in0=self.PAR[:, osg:osg + 1], scalar1=(1.0 - lam_init), scalar2=None, op0=ALU.mult),
                  reads=[SM, self.PAR], writes=[SM])
            Pg.op("pool", lambda e: e.memset(self.R[64:128, 3, :], 0.0), writes=[Rs[3]])
            Pg.op("pool", lambda e: e.memset(self.R[0:64, 4, :], 0.0), writes=[Rs[4]])
            def proj(h):
                self.ws.close_tile()
                self.proj_fm(Rs[0], ht_src, self.HTb, 8, lambda inp, h=h: fm_unit(W(inp)[:, 128 * h:128 * h + 128]),
                             lambda inp, h=h: fm_unit(W(inp)[:, 128 * h + sw]), "rope", self.tab64_d)
                self.proj_fm(Rs[3], ht_src, self.HTb, 8, lambda inp, h=h: fm_unit(W(inp)[:, 1024 + 128 * h:1024 + 128 * h + 128]),
                             lambda inp, h=h: fm_unit(W(inp)[:, 1024 + 128 * h + sw]), "rope", self.tab64_d, dst2=Rs[4])
                self.proj_tm(Rs[6], ht_tile, self.HTb, 8, lambda inp, h=h: fm_unit(W(inp)[:, 2048 + 128 * h:2048 + 128 * h + 128]))

            def attn(h):
                units = [dict(mode="diff", og=0, maps=[
                    dict(terms=[(Rs[3 + jm], 3 + jm, Rs[0], 0)], v=(Rs[6], 6), ones=full_ones) for jm in range(2)])]
                self.attention(units, 64 ** -0.5, qblocks)

            def wo(h):
                self.wo_apply(li, 1, lambda inp, h=h: np.ascontiguousarray(inp["a_w_o"][0][128 * h:128 * h + 128, :]), qblocks)
            self.pipeline(8, proj, attn, wo)
        elif kind == 1:
            W = lambda inp: inp["b_w_qkv"][0]
            sw = swap_idx(128, 64)
            osk = self.par(8, lambda inp, core: np.ascontiguousarray(
                inp["b_sink"][0][(2 * np.arange(8)[None, :] + (np.arange(128)[:, None] >= 64)).astype(np.int64)]))
            Pg.op("act", lambda e: e.activation(out=self.SM[:, 16:24], in_=self.PAR[:, osk:osk + 8], func=AF.Exp), reads=[self.PAR], writes=[self.SM])
            zc = np.zeros((1024, 64), np.float32)
            Pg.op("pool", lambda e: e.memset(self.R[64:128, 3, :], 0.0), writes=[Rs[3]])
            Pg.op("pool", lambda e: e.memset(self.R[0:64, 4, :], 0.0), writes=[Rs[4]])
            def proj(kv):
                self.ws.close_tile()
                for c in range(2):
                    c0 = 256 * kv + 128 * c
                    self.proj_fm(Rs[c], ht_src, self.HTb, 8, lambda inp, c0=c0: fm_unit(W(inp)[:, c0:c0 + 128]),
                                 lambda inp, c0=c0: fm_unit(W(inp)[:, c0 + sw]), "rope", self.tab64_d)
                kcols = 1024 + 64 * kv + (np.arange(128) % 64)
                self.proj_fm(Rs[3], ht_src, self.HTb, 8, lambda inp, kcols=kcols: fm_unit(W(inp)[:, kcols]),
                             lambda inp, kcols=kcols: fm_unit(W(inp)[:, kcols[sw]]), "rope", self.tab64_d, dst2=Rs[4])
                v0 = 1280 + 64 * kv
                self.proj_tm(Rs[6], ht_tile, self.HTb, 8, lambda inp, v0=v0: fm_unit(np.concatenate([W(inp)[:, v0:v0 + 64], zc], axis=1)))
                self.proj_tm(Rs[7], ht_tile, self.HTb, 8, lambda inp, v0=v0: fm_unit(np.concatenate([zc, W(inp)[:, v0:v0 + 64]], axis=1)))

            def attn(kv):
                units = []
                for c in range(2):
                    units.append(dict(mode="sum", og=c, gch=2 * kv + c, maps=[
                        dict(terms=[(Rs[3 + r], 3 + r, Rs[c], c)], v=(Rs[6 + r], 6 + r), ones=self.cbf[:, 1 + r, :]) for r in range(2)]))
                self.attention(units, 64 ** -0.5, qblocks, window=True, zadd=lambda u: self.SM[:, 16 + u["gch"]:17 + u["gch"]])

            def wo(kv):
                self.wo_apply(li, 2, lambda inp, kv=kv: np.ascontiguousarray(
                    inp["b_w_o"][0][256 * kv:256 * kv + 256, :].reshape(2, 128, 1024).transpose(1, 0, 2).reshape(128, 2048)), qblocks)
            self.pipeline(4, proj, attn, wo)
        elif kind == 2:
            Wd_ = lambda inp: inp["c_w_down"][0]
            sw = swap_idx(128, 64)
            oq = self.par(3, lambda inp, core: fmaj(inp["c_q_norm_g"][0]))
            okv = self.par(2, lambda inp, core: fmaj(inp["c_kv_norm_g"][0]))
            self.ws.close_tile()
            krc = 640 + (np.arange(128) % 64)
            recs = [lambda inp, c=c: fm_unit(Wd_(inp)[:, 128 * c:128 * c + 128]) for c in range(5)]
            recs.append(lambda inp: fm_unit(Wd_(inp)[:, krc]))
            recs.append(lambda inp: fm_unit(Wd_(inp)[:, krc[sw]]))
            wu = [self.wunit(1024, r, keep=(i > 0)) for i, r in enumerate(recs)]
            for b in range(5):
                t0, t1 = TB[b]
                w = t1 - t0
                nb = 7 if b > 0 else 6
                bks = [self.banks[i] for i in range(nb)]
                for i in range(nb):
                    slot, off = wu[i]

                    def fn(e, i=i, slot=slot, off=off, w=w, t0=t0, t1=t1):
                        for kc in range(8):
                            ins = e.matmul(self.banks[i][:, 0:w], lhsT=slot[:, off + kc * 128: off + (kc + 1) * 128],
                                           rhs=self.HT[:, kc, t0:t1], start=(kc == 0), stop=(kc == 7))
                        return ins
                    Pg.op("pe", fn, reads=[slot, self.HTb[b]], writes=[bks[i]])
                for (cs, nf, og_) in (((0, 1, 2), 384, oq), ((3, 4), 256, okv)):
                    r = self.rms_stat([bks[c][:, 0:w] for c in cs], w, nf, [bks[c] for c in cs])
                    rap = self.tap(r)[:, 0:w]
                    for ci, c in enumerate(cs):
                        g = self.PAR[:, og_ + ci: og_ + ci + 1]
                        Pg.op("dve", lambda e, c=c, g=g, rap=rap, w=w, t0=t0, t1=t1: e.scalar_tensor_tensor(
                            out=self.HT[:, c, t0:t1], in0=self.banks[c][:, 0:w], scalar=g, in1=rap, op0=ALU.mult, op1=ALU.mult),
                            reads=[bks[c], self.PAR, r], writes=[self.HTb[b]])
                dsta = self.R[:, 5, t0:t1]
                if b == 0:
                    Pg.op("act", lambda e, dsta=dsta, w=w: e.activation(out=dsta, in_=self.banks[5][:, 0:w], func=AF.Copy),
                          reads=[bks[5]], writes=[Rs[5]])
                else:
                    TAB = self.load_tab(self.tab64_d, b)
                    ta, tb_ = self.tmpt(), self.tmpt()
                    Pg.op("dve", lambda e, TAB=TAB, ta=ta, w=w: e.tensor_tensor(out=self.tap(ta)[:, 0:w], in0=self.banks[5][:, 0:w], in1=TAB[:, 0, 0:w], op=ALU.mult),
                          reads=[bks[5], TAB], writes=[ta])
                    Pg.op("dve", lambda e, TAB=TAB, tb_=tb_, w=w: e.tensor_tensor(out=self.tap(tb_)[:, 0:w], in0=self.banks[6][:, 0:w], in1=TAB[:, 1, 0:w], op=ALU.mult),
                          reads=[bks[6], TAB], writes=[tb_])
                    Pg.op("pool", lambda e, ta=ta, tb_=tb_, dsta=dsta, w=w: e.tensor_tensor(out=dsta, in0=self.tap(ta)[:, 0:w], in1=self.tap(tb_)[:, 0:w], op=ALU.add),
                          reads=[ta, tb_], writes=[Rs[5]])
            Wq = lambda inp: inp["c_w_uq"][0]
            Wkv = lambda inp: inp["c_w_ukv"][0]
            cq_src = lambda t0, t1: [self.HT[:, kc, t0:t1] for kc in range(3)]
            ckv_src = lambda t0, t1: [self.HT[:, 3 + kc, t0:t1] for kc in range(2)]
            ckv_tile = lambda kc, tt: self.HT[:, 3 + kc, tt * 128:(tt + 1) * 128]
            zq = np.zeros((384, 64), np.float32)
            def proj(h):
                self.ws.close_tile()
                self.proj_fm(Rs[0], cq_src, self.HTb, 3, lambda inp, h=h: fm_unit(Wq(inp)[:, 192 * h:192 * h + 128]), None, "copy")
                rc = 192 * h + 128 + np.arange(64)
                self.proj_fm(Rs[1], cq_src, self.HTb, 3, lambda inp, rc=rc: fm_unit(np.concatenate([Wq(inp)[:, rc], zq], axis=1)),
                             lambda inp, rc=rc: fm_unit(np.concatenate([Wq(inp)[:, rc[swap_idx(64, 64)]], zq], axis=1)), "rope", self.tab64_d)
                self.proj_fm(Rs[3], ckv_src, self.HTb, 2, lambda inp, h=h: fm_unit(Wkv(inp)[:, 256 * h:256 * h + 128]), None, "copy")
                self.proj_tm(Rs[6], ckv_tile, self.HTb, 2, lambda inp, h=h: fm_unit(Wkv(inp)[:, 256 * h + 128:256 * h + 256]))

            def attn(h):
                units = [dict(mode="sum", og=0, maps=[dict(
                    terms=[(Rs[3], 3, Rs[0], 0), (Rs[5], 5, Rs[1], 1)], v=(Rs[6], 6), ones=full_ones)])]
                self.attention(units, 192 ** -0.5, qblocks)

            def wo(h):
                self.wo_apply(li, 1, lambda inp, h=h: np.ascontiguousarray(inp["c_w_o"][0][128 * h:128 * h + 128, :]), qblocks)
            self.pipeline(8, proj, attn, wo)
        else:
            W = lambda inp: inp["d_w_qkv"][0]
            sw = swap_idx(128, 128)
            ogq = self.par(1, lambda inp, core: np.ascontiguousarray(inp["d_q_norm_g"][0].reshape(128, 1)))
            ogqs = self.par(1, lambda inp, core: np.ascontiguousarray(inp["d_q_norm_g"][0][sw].reshape(128, 1)))
            ogk = self.par(1, lambda inp, core: np.ascontiguousarray(inp["d_k_norm_g"][0].reshape(128, 1)))
            ogks = self.par(1, lambda inp, core: np.ascontiguousarray(inp["d_k_norm_g"][0][sw].reshape(128, 1)))
            def proj(kv):
                self.ws.close_tile()
                for i in range(2):
                    c0 = 128 * (2 * kv + i)
                    self.proj_fm(Rs[i], ht_src, self.HTb, 8, lambda inp, c0=c0: fm_unit(W(inp)[:, c0:c0 + 128]),
                                 lambda inp, c0=c0: fm_unit(W(inp)[:, c0 + sw]), "normrope", self.tab128_d, gpar=ogq, gspar=ogqs)
                k0 = 1024 + 128 * kv
                self.proj_fm(Rs[3], ht_src, self.HTb, 8, lambda inp, k0=k0: fm_unit(W(inp)[:, k0:k0 + 128]),
                             lambda inp, k0=k0: fm_unit(W(inp)[:, k0 + sw]), "normrope", self.tab128_d, gpar=ogk, gspar=ogks)
                v0 = 1536 + 128 * kv
                self.proj_tm(Rs[6], ht_tile, self.HTb, 8, lambda inp, v0=v0: fm_unit(W(inp)[:, v0:v0 + 128]))

            def attn(kv):
                units = [dict(mode="sum", og=i, maps=[dict(terms=[(Rs[3], 3, Rs[i], i)], v=(Rs[6], 6), ones=full_ones)])
                         for i in range(2)]
                self.attention(units, 128 ** -0.5, qblocks)

            def wo(kv):
                self.wo_apply(li, 2, lambda inp, kv=kv: np.ascontiguousarray(
                    inp["d_w_o"][0][256 * kv:256 * kv + 256, :].reshape(2, 128, 1024).transpose(1, 0, 2).reshape(128, 2048)), qblocks)
            self.pipeline(4, proj, attn, wo)
        self.mark("L%d mod2" % li)
        self.modulate(1, qblocks)
        self.mark("L%d ffn" % li)
        self.ffn(li, qblocks)
        if li + 1 < self.nlayers:
            for m2 in (22, 23):
                self.ada_tile(li + 1, m2)

    def final(self):
        Pg = self.P
        ofg = self.par(8, lambda inp, core: fmaj(inp["final_g"]))
        stage = self.TMP.t
        for b in range(1, 5):
            t0, t1 = TB[b]
            w = t1 - t0
            srcs = [self.XT[:, c, t0:t1] for c in range(8)]
            r = self.rms_stat(srcs, w, D, [self.XTb[b]])
            rap = self.tap(r)[:, 0:w]
            for c in range(8):
                g = self.PAR[:, ofg + c: ofg + c + 1]
                Pg.op("dve", lambda e, c=c, g=g, rap=rap, t0=t0, t1=t1: e.scalar_tensor_tensor(
                    out=self.XT[:, c, t0:t1], in0=self.XT[:, c, t0:t1], scalar=g, in1=rap, op0=ALU.mult, op1=ALU.mult),
                    reads=[self.XTb[b], self.PAR, r], writes=[self.XTb[b]])
            for ti in range(4):
                tt0 = t0 + ti * 128
                so = 2 * (ti % 2)
                sb = [self.tmps[so], self.tmps[so + 1]]
                for g2 in range(2):
                    bk = self.bank("g")

                    def tr(e, bk=bk, g2=g2, tt0=tt0):
                        for i in range(4):
                            c = 4 * g2 + i
                            ins = e.transpose(out=bk[:, i * 128:(i + 1) * 128], in_=self.XT[:, c, tt0:tt0 + 128], identity=self.ident[:, :])
                        return ins
                    Pg.op("pe", tr, reads=[self.XTb[b], self.ident], writes=[bk])
                    dst = stage[:, so + g2, :]
                    if g2 == 0:
                        Pg.op("act", lambda e, dst=dst, bk=bk: e.activation(out=dst, in_=bk[:, :], func=AF.Copy), reads=[bk], writes=[sb[g2]])
                    else:
                        Pg.op("dve", lambda e, dst=dst, bk=bk: e.tensor_copy(out=dst, in_=bk[:, :]), reads=[bk], writes=[sb[g2]])
                row0 = tt0 - NCTX
                Pg.dma("sp", "xout%d" % (ti % 2), lambda e, row0=row0, so=so: e.dma_start(out=self.out_d[row0:row0 + 128, :].rearrange("p (a f) -> p a f", a=2),
                                                                                       in_=stage[:, so:so + 2, :]), reads=sb)
        Pg.wait_all("sp", [self.tmps[0], self.tmps[1], self.tmps[2], self.tmps[3]])


_CACHE = {}


def _build(nlayers=DEPTH):
    if nlayers in _CACHE:
        return _CACHE[nlayers]
    nc0 = bass.Bass("TRN2", target_bir_lowering=False)
    g0 = Gen(nc0, 100000, nlayers)
    g0.dry = True
    g0.P.emit = lambda: None
    g0.build()
    n_tiles = len(g0.ws.tiles)
    nc = bass.Bass("TRN2", target_bir_lowering=False)
    g = Gen(nc, n_tiles, nlayers)
    g.build()
    assert len(g.ws.tiles) == n_tiles
    assert g.par_cols <= g.PARW, g.par_cols
    _CACHE[nlayers] = (nc, g)
    return nc, g


def _consts():
    cb = np.zeros((128, 7, 128), np.float32)
    for dh, idx in ((64, 5), (128, 6)):
        sw_ = swap_idx(128, dh)
        cb[sw_, idx, np.arange(128)] = 1.0
    cb[:, 0, :] = 1.0
    cb[:, 1, 0:64] = 1.0
    cb[:, 2, 64:128] = 1.0
    b = np.arange(128)[:, None]
    a = np.arange(128)[None, :]
    cb[:, 3, :] = (a <= b)
    cb[:, 4, :] = (a >= b)
    return cb.astype(ml_dtypes.bfloat16)


def make_in_maps(inp, g, cores):
    wstream = g.ws.materialize(inp)
    tab64 = rope_tables(64)
    tab128 = rope_tables(128)
    ident = np.eye(128, dtype=np.float32)
    cbf = _consts()
    in_maps = []
    for core in cores:
        par = np.zeros((128, g.PARW), np.float32)
        for off, n, rec in g.par_recipes:
            par[:, off:off + n] = rec(inp, core)
        in_maps.append({
            "x": np.ascontiguousarray(inp["x"][core]),
            "ctx": np.ascontiguousarray(inp["ctx"][core]),
            "wstream": wstream,
            "tab64": tab64,
            "tab128": tab128,
            "ident": ident,
            "cbf": cbf,
            "par": par,
        })
    return in_maps


def kernel(nlayers=DEPTH, **inputs):
    inp = {k: np.asarray(v) for k, v in inputs.items()}
    nc, g = _build(nlayers)
    in_maps = make_in_maps(inp, g, range(8))
    res = run_bass_kernel_spmd(nc, in_maps, core_ids=list(range(8)))
    out = np.stack([np.asarray(r["out"]) for r in res.results], axis=0).astype(np.float32)
    return out
```

```python
import math
import numpy as np
import ml_dtypes
import concourse.bass as bass
import concourse.mybir as mybir
from concourse.bass_utils import run_bass_kernel_spmd

F32 = mybir.dt.float32
BF16 = mybir.dt.bfloat16
ALU = mybir.AluOpType
AF = mybir.ActivationFunctionType

ENGS = ("pe", "act", "dve", "pool", "sp")

D = 1024
NCTX = 256
NLAT = 2048
T = NCTX + NLAT
DFF = 2816
DEPTH = 4
EPS = 1e-6
TB = [(0, 256), (256, 768), (768, 1280), (1280, 1792), (1792, 2304)]
TILE_W = 2048
NSLOT = 4


class Buf:
    __slots__ = ("t", "name", "last_write", "readers")

    def __init__(self, t, name=""):
        self.t = t
        self.name = name
        self.last_write = None
        self.readers = {}

    def __getitem__(self, idx):
        return self.t[idx]


class Prog:
    def __init__(self, nc):
        self.nc = nc
        self.ops = {e: [] for e in ENGS}
        self.counts = {e: 0 for e in ENGS}
        self.seen = {e: {} for e in ENGS}
        self.sems = {}
        self.dma_counts = {}
        self._stack = []

    def enter(self, cm):
        v = cm.__enter__()
        self._stack.append(cm)
        return v

    def close(self):
        while self._stack:
            self._stack.pop().__exit__(None, None, None)

    def sbuf(self, name, shape, dtype):
        return Buf(self.enter(self.nc.sbuf_tensor(name, list(shape), dtype)), name)

    def psum(self, name, shape, dtype=F32):
        return Buf(self.enter(self.nc.psum_tensor(name, list(shape), dtype)), name)

    def sem(self, key):
        if key not in self.sems:
            self.sems[key] = self.enter(self.nc.semaphore("s_" + str(key)))
        return self.sems[key]

    def _waits(self, eng, reads, writes):
        waits = {}
        seen = self.seen[eng]

        def need(tok):
            if tok is None:
                return
            k, v = tok
            if eng == "pe" and k == "pe":
                return
            if seen.get(k, 0) >= v:
                return
            if waits.get(k, 0) < v:
                waits[k] = v

        for b in reads:
            need(b.last_write)
        for b in writes:
            need(b.last_write)
            for k, v in b.readers.items():
                need((k, v))
        for k, v in waits.items():
            seen[k] = v
        return list(waits.items())

    def op(self, eng, fn, reads=(), writes=()):
        waits = self._waits(eng, reads, writes)
        self.counts[eng] += 1
        n = self.counts[eng]
        self.ops[eng].append((waits, fn, (eng, 1)))
        for b in reads:
            b.readers[eng] = n
        for b in writes:
            b.last_write = (eng, n)
            b.readers = {}

    def dma(self, eng, semkey, fn, reads=(), writes=()):
        waits = self._waits(eng, reads, writes)
        self.sem(semkey)
        self.dma_counts[semkey] = self.dma_counts.get(semkey, 0) + 16
        n = self.dma_counts[semkey]
        self.ops[eng].append((waits, fn, (semkey, 16)))
        for b in reads:
            b.readers[semkey] = n
        for b in writes:
            b.last_write = (semkey, n)
            b.readers = {}

    def wait_all(self, eng, bufs):
        waits = self._waits(eng, bufs, bufs)
        if waits:
            self.ops[eng].append((waits, None, None))

    def emit(self):
        nc = self.nc
        for e in ENGS:
            self.sem(e)
        prog = self

        class _Cnt:
            def __init__(self, e):
                self.e = e
                self.n = 0

            def matmul(self, *a, **k):
                self.n += 1
                return self.e.matmul(*a, **k)

            def transpose(self, *a, **k):
                self.n += 1
                return self.e.transpose(*a, **k)

        def replay(engine, name):
            if name == "pe":
                engine = _Cnt(engine)
                prog.pe_instr_at = []
            for waits, fn, inc in prog.ops[name]:
                if name == "pe":
                    prog.pe_instr_at.append(engine.n)
                    for k, v in waits:
                        engine.e.wait_ge(prog.sems[k], v)
                else:
                    for k, v in waits:
                        engine.wait_ge(prog.sems[k], v)
                if fn is None:
                    continue
                ins = fn(engine)
                ins.then_inc(prog.sems[inc[0]], inc[1])

        with nc.Block() as block:
            @block.tensor
            def _(e):
                replay(e, "pe")

            @block.scalar
            def _(e):
                replay(e, "act")

            @block.vector
            def _(e):
                replay(e, "dve")

            @block.gpsimd
            def _(e):
                replay(e, "pool")

            @block.sync
            def _(e):
                replay(e, "sp")


def fm_unit(wcols):
    k, m = wcols.shape
    n_kc = k // 128
    return np.ascontiguousarray(wcols.reshape(n_kc, 128, m).transpose(1, 0, 2).reshape(128, n_kc * m))


def fmaj(v):
    return np.ascontiguousarray(np.asarray(v, np.float32).reshape(-1, 128).T)


def swap_idx(n, dh):
    p = np.arange(n)
    return (p // dh) * dh + ((p % dh) + dh // 2) % dh


def rope_tables(rot_dim):
    n_rows = NLAT // 64
    rows = np.repeat(np.arange(n_rows, dtype=np.float32), 64)
    cols = np.tile(np.arange(64, dtype=np.float32), n_rows)
    n_freq = rot_dim // 4
    inv_freq = (np.float32(10000.0) ** (-np.arange(n_freq, dtype=np.float32) / np.float32(n_freq))).astype(np.float32)
    ang = np.concatenate([rows[:, None] * inv_freq, cols[:, None] * inv_freq], axis=-1).astype(np.float32)
    cos, sin = np.cos(ang).astype(np.float32), np.sin(ang).astype(np.float32)
    p = np.arange(128)
    d = p % rot_dim
    half = rot_dim // 2
    fi = d % half
    sign = np.where(d < half, -1.0, 1.0).astype(np.float32)
    tab = np.zeros((128, 2, NLAT), np.float32)
    tab[:, 0, :] = cos[:, fi].T
    tab[:, 1, :] = (sin[:, fi] * sign[None, :]).T
    return tab


class WStream:
    def __init__(self):
        self.tiles = []
        self.cur = TILE_W

    def add(self, n, recipe):
        assert n <= TILE_W
        if self.cur + n > TILE_W:
            self.tiles.append([])
            self.cur = 0
        off = self.cur
        self.tiles[-1].append((off, n, recipe))
        self.cur += n
        return len(self.tiles) - 1, off

    def close_tile(self):
        self.cur = TILE_W

    def materialize(self, inp):
        arr = np.zeros((len(self.tiles), 128, TILE_W), np.float32)
        for ti, units in enumerate(self.tiles):
            for off, n, recipe in units:
                a = recipe(inp)
                assert a.shape == (128, n), (a.shape, n)
                arr[ti, :, off:off + n] = a
        return arr


class Gen:
    def __init__(self, nc, n_tiles, nlayers=DEPTH):
        self.nc = nc
        self.P = Prog(nc)
        self.ws = WStream()
        self.n_tiles = n_tiles
        self.nlayers = nlayers
        self.par_cols = 0
        self.dry = False
        self.oldest = 0
        self.par_recipes = []
        self.marks = []
        self.pending_fin = None
        self.const_recipes = {}

    def mark(self, name):
        self.marks.append((name, len(self.P.ops["pe"])))

    def par(self, n, recipe):
        off = self.par_cols
        self.par_cols += n
        self.par_recipes.append((off, n, recipe))
        return off

    def wunit(self, n, recipe, keep=False):
        ti, off = self.ws.add(n, recipe)
        if not keep:
            self.oldest = ti
        assert ti - self.oldest < NSLOT
        self.ensure_tile(ti)
        return self.slots[ti % NSLOT], off

    def ensure_tile(self, ti):
        P = self.P
        hi = min(self.oldest + NSLOT - 1, self.n_tiles - 1)
        assert ti <= hi or self.dry
        while self.next_tile <= hi:
            t = self.next_tile
            slot = self.slots[t % NSLOT]
            if not self.dry:
                src = self.w_d[t]
                P.dma("pool", "w%d" % (t % NSLOT),
                      lambda e, slot=slot, src=src: e.dma_start(out=slot[:, :], in_=src), writes=[slot])
            self.next_tile += 1

    def bank(self, pool):
        lst, idx = self.bpools[pool]
        b = lst[idx[0] % len(lst)]
        idx[0] += 1
        return b

    def pt(self):
        b = self.pts[self.pt_i % len(self.pts)]
        self.pt_i += 1
        return b

    def tmpt(self):
        b = self.tmps[self.tmp_i % len(self.tmps)]
        self.tmp_i += 1
        return b

    def build(self):
        nc, P = self.nc, self.P
        nl = self.nlayers
        self.x_d = nc.dram_tensor("x", [NLAT, D], F32, kind="ExternalInput").ap()
        self.ctx_d = nc.dram_tensor("ctx", [NCTX, D], F32, kind="ExternalInput").ap()
        self.w_d = nc.dram_tensor("wstream", [1 if self.dry else self.n_tiles, 128, TILE_W], F32, kind="ExternalInput").ap()
        self.tab64_d = nc.dram_tensor("tab64", [128, 2, NLAT], F32, kind="ExternalInput").ap()
        self.tab128_d = nc.dram_tensor("tab128", [128, 2, NLAT], F32, kind="ExternalInput").ap()
        self.ident_d = nc.dram_tensor("ident", [128, 128], F32, kind="ExternalInput").ap()
        self.cbf_d = nc.dram_tensor("cbf", [128, 7, 128], BF16, kind="ExternalInput").ap()
        self.out_d = nc.dram_tensor("out", [NLAT, D], F32, kind="ExternalOutput").ap()
        self.next_tile = 0

        self.XT = P.sbuf("XT", [128, 8, T], F32)
        self.XTb = [Buf(self.XT.t, "XT%d" % b) for b in range(5)]
        self.HT = P.sbuf("HT", [128, 8, T], BF16)
        self.HTb = [Buf(self.HT.t, "HT%d" % b) for b in range(5)]
        self.R = P.sbuf("R", [128, 8, T], BF16)
        self.Rs = [Buf(self.R.t, "R%d" % s) for s in range(8)]
        self.OG = P.sbuf("OG", [128, 2, T], BF16)
        self.OGs = [Buf(self.OG.t, "OG%d" % s) for s in range(2)]
        self.TABS = [P.sbuf("TAB%d" % i, [128, 2, 512], F32) for i in range(2)]
        self.tab_i = 0
        self.slots = [P.sbuf("wslot%d" % i, [128, TILE_W], BF16) for i in range(NSLOT)]
        self.pts = [P.sbuf("pt%d" % i, [128, 512], BF16) for i in range(5)]
        self.pt_i = 0
        self.TMP = P.sbuf("TMP", [128, 6, 512], F32)
        self.tmps = [Buf(self.TMP.t, "tmp%d" % i) for i in range(6)]
        self.tmp_i = 0
        self.RST = P.sbuf("RST", [128, 512], F32)
        self.RST2 = P.sbuf("RST2", [128, 512], F32)
        self.epsB = P.sbuf("epsB", [128, 2], F32)
        self.epsb = self.epsB.t
        self.ident = P.sbuf("identS", [128, 128], F32)
        self.cbf = P.sbuf("cbfS", [128, 7, 128], BF16)
        self.MODS = [P.sbuf("MOD%d" % i, [128, 48, 2], F32) for i in range(2)]
        self.AMS = [P.sbuf("AM%d" % i, [128, 2, 8, 2], F32) for i in range(2)]
        self.MOD, self.AM = self.MODS[0], self.AMS[0]
        self.SC = P.sbuf("SCb", [128, 8, 2], BF16)
        self.SM = P.sbuf("SM", [128, 64], F32)
        self.banks = [P.psum("bank%d" % i, [128, 512]) for i in range(8)]
        self.bpools = {
            "g": (self.banks[0:4], [0]),
            "s": (self.banks[0:3], [0]),
            "o": (self.banks[3:7], [0]),
            "f": (self.banks[0:7], [0]),
        }
        self.STAT = self.banks[7]

        self.PARW = 1280
        self.PAR = P.sbuf("PAR", [128, self.PARW], F32)
        self.par_d = nc.dram_tensor("par", [128, self.PARW], F32, kind="ExternalInput").ap()

        P.dma("sp", "c0", lambda e: e.dma_start(out=self.PAR[:, :], in_=self.par_d), writes=[self.PAR])
        P.dma("sp", "c1", lambda e: e.dma_start(out=self.ident[:, :], in_=self.ident_d), writes=[self.ident])
        P.dma("sp", "c2", lambda e: e.dma_start(out=self.cbf[:, :, :], in_=self.cbf_d), writes=[self.cbf])
        self.ones = self.cbf[:, 0, :]
        P.op("dve", lambda e: e.memset(self.epsb[:, :], EPS), writes=[self.epsB])
        o_c = self.par(8, lambda inp, core: fmaj(inp["c"][core]))
        o_cc = self.par(8, lambda inp, core: fmaj(inp["c_ctx"]))
        P.op("act", lambda e: e.activation(out=self.SC[:, :, 0], in_=self.PAR[:, o_c:o_c + 8], func=AF.Silu),
             reads=[self.PAR], writes=[self.SC])
        P.op("act", lambda e: e.activation(out=self.SC[:, :, 1], in_=self.PAR[:, o_cc:o_cc + 8], func=AF.Silu),
             reads=[self.PAR], writes=[self.SC])
        self.mark("load_x")
        self.load_x()
        for li in range(nl):
            self.layer(li)
        self.mark("final")
        self.final()
        self.mark("end")
        P.emit()
        P.close()

    def load_x(self):
        P = self
        Pg = self.P
        stage = self.TMP.t
        for tt in range(18):
            so = 2 * (tt % 2)
            sb = [self.tmps[so], self.tmps[so + 1]]
            if tt < 2:
                src = self.ctx_d[tt * 128:(tt + 1) * 128, :]
            else:
                src = self.x_d[(tt - 2) * 128:(tt - 1) * 128, :]
            b = 0 if tt < 2 else 1 + (tt - 2) // 4
            Pg.dma("sp", "xin%d" % (tt % 2), lambda e, src=src, so=so: e.dma_start(out=stage[:, so:so + 2, :], in_=src.rearrange("p (a f) -> p a f", a=2)),
                   writes=sb)
            for g in range(2):
                bk = self.bank("g")

                def tr(e, g=g, bk=bk, so=so):
                    for i in range(4):
                        c = 4 * g + i
                        ins = e.transpose(out=bk[:, i * 128:(i + 1) * 128], in_=stage[:, so + c // 4, (c % 4) * 128:(c % 4 + 1) * 128],
                                          identity=self.ident[:, :])
                    return ins
                Pg.op("pe", tr, reads=sb + [self.ident], writes=[bk])
                eng = "act" if g == 0 else "dve"
                dst = self.XT[:, 4 * g:4 * g + 4, tt * 128:(tt + 1) * 128]
                srcp = bk[:, :].rearrange("p (a f) -> p a f", a=4)
                if eng == "act":
                    Pg.op("act", lambda e, dst=dst, srcp=srcp: e.activation(out=dst, in_=srcp, func=AF.Copy),
                          reads=[bk], writes=[self.XTb[b]])
                else:
                    Pg.op("dve", lambda e, dst=dst, srcp=srcp: e.tensor_copy(out=dst, in_=srcp),
                          reads=[bk], writes=[self.XTb[b]])
            self.ada_tile(0, tt)

    def rms_stat(self, srcs, width, n_feat, reads):
        Pg = self.P
        n = len(srcs)
        for i, s in enumerate(srcs):
            sq = self.pt()
            Pg.op("act", lambda e, s=s, sq=sq: e.activation(out=sq[:, 0:width], in_=s, func=AF.Square),
                  reads=reads, writes=[sq])
            Pg.op("pe", lambda e, sq=sq, i=i: e.matmul(self.STAT[:, 0:width], lhsT=self.ones, rhs=sq[:, 0:width],
                                                     start=(i == 0), stop=(i == n - 1)),
                  reads=[sq, self.cbf], writes=[self.STAT])
        r = self.RST
        Pg.op("act", lambda e: e.activation(out=r[:, 0:width], in_=self.STAT[:, 0:width], func=AF.Ln,
                                            bias=self.epsb[:, 0:1], scale=1.0 / n_feat),
              reads=[self.STAT, self.epsB], writes=[r])
        Pg.op("act", lambda e: e.activation(out=r[:, 0:width], in_=r[:, 0:width], func=AF.Exp, scale=-0.5),
              reads=[r], writes=[r])
        return r

    def tap(self, r):
        if r is self.RST:
            return self.RST[:, :]
        return self.TMP.t[:, self.tmps.index(r), :]

    def modulate(self, which, blocks):
        Pg = self.P
        blocks = list(blocks)
        rt = [self.RST, self.RST2]

        def stage1(i):
            b = blocks[i]
            t0, t1 = TB[b]
            w = t1 - t0
            for c in range(8):
                sq = self.pt()
                xin = self.XT[:, c, t0:t1]
                if c % 2 == 0:
                    Pg.op("act", lambda e, sq=sq, xin=xin, w=w: e.activation(out=sq[:, 0:w], in_=xin, func=AF.Square),
                          reads=[self.XTb[b]], writes=[sq])
                else:
                    Pg.op("pool", lambda e, sq=sq, xin=xin, w=w: e.tensor_tensor(out=sq[:, 0:w], in0=xin, in1=xin, op=ALU.mult),
                          reads=[self.XTb[b]], writes=[sq])
                Pg.op("pe", lambda e, sq=sq, c=c, w=w: e.matmul(self.STAT[:, 0:w], lhsT=self.ones, rhs=sq[:, 0:w],
                                                              start=(c == 0), stop=(c == 7)),
                      reads=[sq, self.cbf], writes=[self.STAT])
            r = rt[i % 2]
            Pg.op("act", lambda e, r=r, w=w: e.activation(out=r[:, 0:w], in_=self.STAT[:, 0:w], func=AF.Sqrt,
                                                        bias=self.epsb[:, 0:1], scale=1.0 / D),
                  reads=[self.STAT, self.epsB], writes=[r])
            Pg.op("dve", lambda e, r=r, w=w: e.reciprocal(out=r[:, 0:w], in_=r[:, 0:w]), reads=[r], writes=[r])
            return r

        def stage2(i, r):
            b = blocks[i]
            t0, t1 = TB[b]
            w = t1 - t0
            j = 1 if b == 0 else 0
            rap = r[:, 0:w]
            for c in range(8):
                tm = self.tmpt()
                tma = self.tap(tm)[:, 0:w]
                am = self.AM[:, which, c, j:j + 1]
                sh = self.MOD[:, (3 * which) * 8 + c, j:j + 1]
                xin = self.XT[:, c, t0:t1]
                hout = self.HT[:, c, t0:t1]
                Pg.op("dve", lambda e, tma=tma, xin=xin, am=am, rap=rap: e.scalar_tensor_tensor(
                    out=tma, in0=xin, scalar=am, in1=rap, op0=ALU.mult, op1=ALU.mult),
                    reads=[self.XTb[b], self.AM, r], writes=[tm])
                Pg.op("act", lambda e, tma=tma, hout=hout, sh=sh: e.activation(
                    out=hout, in_=tma, func=AF.Identity, bias=sh, scale=1.0),
                    reads=[tm, self.MOD], writes=[self.HTb[b]])

        rs = {0: stage1(0)}
        for i in range(len(blocks)):
            if i + 1 < len(blocks):
                rs[i + 1] = stage1(i + 1)
            stage2(i, rs[i])

    def ada_tile(self, li, m2):
        Pg = self.P
        bk = self.banks[7]
        self.ws.close_tile()
        units = []
        for m in (2 * m2, 2 * m2 + 1):
            slot, off = self.wunit(1024, lambda inp, m=m: fm_unit(inp["ada_w"][li][:, m * 128:(m + 1) * 128]), keep=(m % 2 == 1))
            units.append((m, slot, off))

        def fn(e, units=units):
            for m, slot, off in units:
                for kc in range(8):
                    ins = e.matmul(bk[:, 2 * m:2 * m + 2], lhsT=slot[:, off + kc * 128: off + (kc + 1) * 128],
                                   rhs=self.SC[:, kc, :], start=(kc == 0), stop=(kc == 7))
            return ins
        Pg.op("pe", fn, reads=[units[0][1], self.SC], writes=[bk])

    def ada_finish(self, li):
        Pg = self.P
        ob = self.par(48, lambda inp, core: fmaj(inp["ada_b"][li]))
        og1 = self.par(8, lambda inp, core: fmaj(inp["norm1_g"][li]))
        og2 = self.par(8, lambda inp, core: fmaj(inp["norm2_g"][li]))
        bk = self.banks[7]
        MOD = self.MODS[li % 2]
        AM = self.AMS[li % 2]
        self.MOD, self.AM = MOD, AM
        for j in range(2):
            Pg.op("dve", lambda e, j=j: e.tensor_tensor(out=MOD[:, :, j], in0=bk[:, 0:96].rearrange("p (m j) -> p m j", j=2)[:, :, j],
                                                        in1=self.PAR[:, ob:ob + 48], op=ALU.add),
                  reads=[bk, self.PAR], writes=[MOD])
        for which, og in ((0, og1), (1, og2)):
            for j in range(2):
                sc = MOD[:, (3 * which + 1) * 8:(3 * which + 2) * 8, j]
                Pg.op("dve", lambda e, which=which, j=j, sc=sc, og=og: e.scalar_tensor_tensor(
                    out=AM[:, which, :, j], in0=sc, scalar=1.0, in1=self.PAR[:, og:og + 8], op0=ALU.add, op1=ALU.mult),
                    reads=[MOD, self.PAR], writes=[AM])

    def load_tab(self, tab_d, b):
        t0, t1 = TB[b]
        self.tab_i += 1
        tab = self.TABS[self.tab_i % 2]
        self.TAB = tab
        self.P.dma("sp", "tab%d" % (self.tab_i % 2), lambda e: e.dma_start(out=tab[:, :, :], in_=tab_d[:, :, t0 - NCTX:t1 - NCTX]),
                   writes=[tab])
        return tab

    def proj_fm(self, dst, src_chunks, src_bufs, n_kc, rec, rec_swap, mode, tab_d=None, blocks=range(5),
                gpar=None, gspar=None, dst2=None):
        Pg = self.P
        slot, off = self.wunit(n_kc * 128, rec)
        perm = self.cbf[:, 6 if mode == "normrope" else 5, :]
        si = self.Rs.index(dst)
        si2 = self.Rs.index(dst2) if dst2 is not None else None
        def body(b):
            t0, t1 = TB[b]
            w = t1 - t0
            rhs = src_chunks(t0, t1)
            bk = self.bank("g")

            def fn(e, slot=slot, off=off, rhs=rhs, bk=bk, w=w):
                for kc in range(n_kc):
                    ins = e.matmul(bk[:, 0:w], lhsT=slot[:, off + kc * 128: off + (kc + 1) * 128], rhs=rhs[kc],
                                   start=(kc == 0), stop=(kc == n_kc - 1))
                return ins
            Pg.op("pe", fn, reads=[slot, src_bufs[b]], writes=[bk])
            dsta = self.R[:, si, t0:t1]
            roped = (mode in ("rope", "normrope")) and b > 0
            if roped:
                TAB = self.load_tab(tab_d, b)
            yield
            if roped:
                qb = self.pt()
                Pg.op("dve", lambda e: e.tensor_copy(out=qb[:, 0:w], in_=bk[:, 0:w]), reads=[bk], writes=[qb])
                bk2 = self.bank("g")
                Pg.op("pe", lambda e: e.matmul(bk2[:, 0:w], lhsT=perm, rhs=qb[:, 0:w], start=True, stop=True),
                      reads=[qb, self.cbf], writes=[bk2])
            if dst2 is not None:
                assert mode == "rope"
                dlo = self.R[0:64, si, t0:t1]
                dhi = self.R[64:128, si2, t0:t1]
                if b == 0:
                    Pg.op("act", lambda e, dlo=dlo, bk=bk, w=w: e.activation(out=dlo, in_=bk[0:64, 0:w], func=AF.Copy),
                          reads=[bk], writes=[dst])
                    Pg.op("act", lambda e, dhi=dhi, bk=bk, w=w: e.activation(out=dhi, in_=bk[64:128, 0:w], func=AF.Copy),
                          reads=[bk], writes=[dst2])
                else:
                    ta, tb_ = self.tmpt(), self.tmpt()
                    Pg.op("dve", lambda e, TAB=TAB, ta=ta, bk=bk, w=w: e.tensor_tensor(out=self.tap(ta)[:, 0:w], in0=bk[:, 0:w], in1=TAB[:, 0, 0:w], op=ALU.mult),
                          reads=[bk, TAB], writes=[ta])
                    Pg.op("dve", lambda e, TAB=TAB, tb_=tb_, bk2=bk2, w=w: e.tensor_tensor(out=self.tap(tb_)[:, 0:w], in0=bk2[:, 0:w], in1=TAB[:, 1, 0:w], op=ALU.mult),
                          reads=[bk2, TAB], writes=[tb_])
                    Pg.op("dve", lambda e, ta=ta, tb_=tb_, dlo=dlo, w=w: e.tensor_tensor(out=dlo, in0=self.tap(ta)[0:64, 0:w], in1=self.tap(tb_)[0:64, 0:w], op=ALU.add),
                          reads=[ta, tb_], writes=[dst])
                    Pg.op("pool", lambda e, ta=ta, tb_=tb_, dhi=dhi, w=w: e.tensor_tensor(out=dhi, in0=self.tap(ta)[64:128, 0:w], in1=self.tap(tb_)[64:128, 0:w], op=ALU.add),
                          reads=[ta, tb_], writes=[dst2])
            elif mode == "copy" or (mode == "rope" and b == 0):
                Pg.op("act", lambda e, dsta=dsta, bk=bk, w=w: e.activation(out=dsta, in_=bk[:, 0:w], func=AF.Copy),
                      reads=[bk], writes=[dst])
            elif mode == "rope":
                ta, tb_ = self.tmpt(), self.tmpt()
                Pg.op("dve", lambda e, TAB=TAB, ta=ta, bk=bk, w=w: e.tensor_tensor(out=self.tap(ta)[:, 0:w], in0=bk[:, 0:w], in1=TAB[:, 0, 0:w], op=ALU.mult),
                      reads=[bk, TAB], writes=[ta])
                Pg.op("dve", lambda e, TAB=TAB, tb_=tb_, bk2=bk2, w=w: e.tensor_tensor(out=self.tap(tb_)[:, 0:w], in0=bk2[:, 0:w], in1=TAB[:, 1, 0:w], op=ALU.mult),
                      reads=[bk2, TAB], writes=[tb_])
                Pg.op("dve", lambda e, ta=ta, tb_=tb_, dsta=dsta, w=w: e.tensor_tensor(out=dsta, in0=self.tap(ta)[:, 0:w], in1=self.tap(tb_)[:, 0:w], op=ALU.add),
                      reads=[ta, tb_], writes=[dst])
            elif mode == "normrope":
                r = self.rms_stat([bk[:, 0:w]], w, 128, [bk])
                rap = self.tap(r)[:, 0:w]
                g = self.PAR[:, gpar:gpar + 1]
                if b == 0:
                    Pg.op("dve", lambda e, dsta=dsta, bk=bk, w=w, g=g, rap=rap: e.scalar_tensor_tensor(
                        out=dsta, in0=bk[:, 0:w], scalar=g, in1=rap, op0=ALU.mult, op1=ALU.mult),
                        reads=[bk, self.PAR, r], writes=[dst])
                else:
                    gs = self.PAR[:, gspar:gspar + 1]
                    ta, tb_ = self.tmpt(), self.tmpt()
                    Pg.op("dve", lambda e, ta=ta, bk=bk, w=w, g=g, TAB=TAB: e.scalar_tensor_tensor(
                        out=self.tap(ta)[:, 0:w], in0=bk[:, 0:w], scalar=g, in1=TAB[:, 0, 0:w], op0=ALU.mult, op1=ALU.mult),
                        reads=[bk, TAB, self.PAR], writes=[ta])
                    Pg.op("dve", lambda e, tb_=tb_, bk2=bk2, w=w, gs=gs, TAB=TAB: e.scalar_tensor_tensor(
                        out=self.tap(tb_)[:, 0:w], in0=bk2[:, 0:w], scalar=gs, in1=TAB[:, 1, 0:w], op0=ALU.mult, op1=ALU.mult),
                        reads=[bk2, TAB, self.PAR], writes=[tb_])
                    Pg.op("dve", lambda e, ta=ta, tb_=tb_, w=w: e.tensor_tensor(out=self.tap(ta)[:, 0:w], in0=self.tap(ta)[:, 0:w], in1=self.tap(tb_)[:, 0:w], op=ALU.add),
                          reads=[ta, tb_], writes=[ta])
                    Pg.op("dve", lambda e, ta=ta, dsta=dsta, w=w, rap=rap: e.tensor_tensor(out=dsta, in0=self.tap(ta)[:, 0:w], in1=rap, op=ALU.mult),
                          reads=[ta, r], writes=[dst])

        prev = None
        for b in blocks:
            g_ = body(b)
            next(g_)
            if prev is not None:
                for _ in prev:
                    pass
            prev = g_
        if prev is not None:
            for _ in prev:
                pass

    def proj_tm(self, dst, src_tile, src_bufs, n_kc, rec, dst2=None):
        Pg = self.P
        slot, off = self.wunit(n_kc * 128, rec)
        si = self.Rs.index(dst)
        for q in range(5):
            tts = list(range(4 * q, min(4 * q + 4, 18)))
            bk = self.bank("g")
            bset = sorted(set((0 if tt < 2 else 1 + (tt - 2) // 4) for tt in tts))

            def fn(e, tts=tts, bk=bk, slot=slot, off=off):
                for i, tt in enumerate(tts):
                    for kc in range(n_kc):
                        ins = e.matmul(bk[:, i * 128:(i + 1) * 128], lhsT=src_tile(kc, tt),
                                       rhs=slot[:, off + kc * 128: off + (kc + 1) * 128], start=(kc == 0), stop=(kc == n_kc - 1))
                return ins
            Pg.op("pe", fn, reads=[slot] + [src_bufs[b] for b in bset], writes=[bk])
            n = len(tts)
            dsta = self.R[:, si, tts[0] * 128:(tts[0] + n) * 128]
            if dst2 is not None:
                si2 = self.Rs.index(dst2)
                srcv = bk[:, 0:n * 128].rearrange("p (t d) -> p t d", d=128)
                dA = dsta.rearrange("p (t d) -> p t d", d=128)[:, :, 0:64]
                dB = self.R[:, si2, tts[0] * 128:(tts[0] + n) * 128].rearrange("p (t d) -> p t d", d=128)[:, :, 64:128]
                Pg.op("dve", lambda e, dA=dA, srcv=srcv: e.tensor_copy(out=dA, in_=srcv[:, :, 0:64]),
                      reads=[bk], writes=[dst])
                Pg.op("dve", lambda e, dB=dB, srcv=srcv: e.tensor_copy(out=dB, in_=srcv[:, :, 64:128]),
                      reads=[bk], writes=[dst2])
                continue
            Pg.op("act", lambda e, dsta=dsta, bk=bk, n=n: e.activation(out=dsta, in_=bk[:, 0:n * 128], func=AF.Copy),
                  reads=[bk], writes=[dst])

    def attention(self, units, scale, qblocks, window=False, zadd=None):
        Pg = self.P
        LOOK = 2
        for b in qblocks:
            t0, t1 = TB[b]
            w = t1 - t0
            if b == 0:
                ktl = [(0, 0, w, []), (1, 0, w, [])]
            elif not window:
                ktl = [(kt, 0, w, []) for kt in range(18)]
            else:
                ktl = [(0, 0, w, []), (1, 0, w, [])]
                i0 = (t0 - NCTX) // 128
                for jk in range(max(i0 - 1, 0), min(i0 + 5, 16)):
                    lo = max(jk - 1, i0)
                    hi = min(jk + 1, i0 + 3)
                    masks = []
                    if lo <= jk + 1 <= hi:
                        masks.append((3, (jk + 1 - i0) * 128, (jk + 2 - i0) * 128))
                    if lo <= jk - 1 <= hi:
                        masks.append((4, (jk - 1 - i0) * 128, (jk - i0) * 128))
                    ktl.append((2 + jk, (lo - i0) * 128, (hi + 1 - i0) * 128, masks))
            for u in units:
                diff = u["mode"] == "diff"
                passes = [[m] for m in u["maps"]] if diff else [u["maps"]]
                an = []
                for pi_, maps in enumerate(passes):
                    O = self.bank("o")
                    Z = self.bank("o")

                    def qk(ki, mi, maps=maps):
                        kt, c0, c1, _ = ktl[ki]
                        bk = self.bank("s")
                        terms = maps[mi]["terms"]

                        def fn(e, bk=bk, terms=terms, kt=kt, c0=c0, c1=c1, t0=t0):
                            for i, (kbuf, ks, qbuf, qs) in enumerate(terms):
                                ins = e.matmul(bk[:, c0:c1], lhsT=self.R[:, ks, kt * 128:(kt + 1) * 128],
                                               rhs=self.R[:, qs, t0 + c0:t0 + c1], start=(i == 0), stop=(i == len(terms) - 1))
                            return ins
                        rd = []
                        for (kbuf, ks, qbuf, qs) in terms:
                            rd += [kbuf, qbuf]
                        Pg.op("pe", fn, reads=rd, writes=[bk])
                        return bk

                    seq = [(ki, mi) for ki in range(len(ktl)) for mi in range(len(maps))]
                    sb = {}
                    nxt = 0
                    for si_, (ki, mi) in enumerate(seq):
                        while nxt < len(seq) and nxt <= si_ + LOOK:
                            sb[seq[nxt]] = qk(*seq[nxt])
                            nxt += 1
                        kt, c0, c1, masks = ktl[ki]
                        m = maps[mi]
                        bk = sb.pop((ki, mi))
                        pt = self.pt()
                        Pg.op("act", lambda e, pt=pt, bk=bk, c0=c0, c1=c1: e.activation(out=pt[:, c0:c1], in_=bk[:, c0:c1], func=AF.Exp, scale=scale),
                              reads=[bk], writes=[pt])
                        for (mk, a0, a1) in masks:
                            Pg.op("pool", lambda e, pt=pt, mk=mk, a0=a0, a1=a1: e.tensor_tensor(out=pt[:, a0:a1], in0=pt[:, a0:a1], in1=self.cbf[:, mk, :], op=ALU.mult),
                                  reads=[pt, self.cbf], writes=[pt])
                        vbuf, vs = m["v"]
                        onesap = m["ones"]
                        st = (si_ == 0)
                        last = (si_ == len(seq) - 1)

                        def pv(e, pt=pt, c0=c0, c1=c1, vs=vs, kt=kt, onesap=onesap, st=st, last=last, O=O, Z=Z):
                            e.matmul(O[:, c0:c1], lhsT=self.R[:, vs, kt * 128:(kt + 1) * 128], rhs=pt[:, c0:c1], start=st, stop=last)
                            return e.matmul(Z[:, c0:c1], lhsT=onesap, rhs=pt[:, c0:c1], start=st, stop=last)
                        Pg.op("pe", pv, reads=[pt, vbuf, self.cbf], writes=[O, Z])
                    rz = self.tmpt()
                    rza = self.tap(rz)[:, 0:w]
                    if zadd is not None:
                        za = zadd(u)
                        Pg.op("dve", lambda e, rza=rza, Z=Z, za=za, w=w: e.tensor_scalar(out=rza, in0=Z[:, 0:w], scalar1=za, scalar2=None, op0=ALU.add),
                              reads=[Z, self.SM], writes=[rz])
                        Pg.op("dve", lambda e, rza=rza: e.reciprocal(out=rza, in_=rza), reads=[rz], writes=[rz])
                    else:
                        Pg.op("dve", lambda e, rza=rza, Z=Z, w=w: e.reciprocal(out=rza, in_=Z[:, 0:w]), reads=[Z], writes=[rz])
                    og = u["og"]
                    dst = self.OG[:, og, t0:t1]
                    if not diff:
                        Pg.op("dve", lambda e, rza=rza, O=O, dst=dst, w=w: e.tensor_tensor(out=dst, in0=O[:, 0:w], in1=rza, op=ALU.mult),
                              reads=[O, rz], writes=[self.OGs[og]])
                    else:
                        Pg.op("dve", lambda e, rza=rza, O=O, w=w: e.tensor_tensor(out=rza, in0=O[:, 0:w], in1=rza, op=ALU.mult),
                              reads=[O, rz], writes=[rz])
                        an.append(rz)
                    if pi_ == 0 and self.pending_fin is not None:
                        self.pending_fin()
                        self.pending_fin = None
                if diff:
                    a0a_, a1a_ = self.tap(an[0])[:, 0:w], self.tap(an[1])[:, 0:w]
                    nl0_ = self.SM[:, 8:9]
                    Pg.op("dve", lambda e, a0a_=a0a_, a1a_=a1a_, nl0_=nl0_: e.scalar_tensor_tensor(out=a0a_, in0=a1a_, scalar=nl0_, in1=a0a_, op0=ALU.mult, op1=ALU.add),
                          reads=[an[0], an[1], self.SM], writes=[an[0]])

                    def fin(an=an, w=w, dst=dst, og=og):
                        a0a = self.tap(an[0])[:, 0:w]
                        r = self.rms_stat([a0a], w, 128, [an[0]])
                        rap = self.tap(r)[:, 0:w]
                        sg = self.SM[:, 9:10]
                        Pg.op("dve", lambda e: e.scalar_tensor_tensor(out=dst, in0=a0a, scalar=sg, in1=rap, op0=ALU.mult, op1=ALU.mult),
                              reads=[an[0], self.SM, r], writes=[self.OGs[og]])
                    self.pending_fin = fin
        if self.pending_fin is not None:
            self.pending_fin()
            self.pending_fin = None

    def wo_apply(self, li, n_u, rec, blocks):
        Pg = self.P
        slot, off = self.wunit(n_u * 1024, rec)
        for b in blocks:
            t0, t1 = TB[b]
            w = t1 - t0
            j = 1 if b == 0 else 0
            for c in range(8):
                bk = self.bank("g")

                def fn(e, bk=bk, c=c, w=w, t0=t0, t1=t1):
                    for u in range(n_u):
                        ins = e.matmul(bk[:, 0:w], lhsT=slot[:, off + u * 1024 + c * 128: off + u * 1024 + (c + 1) * 128],
                                       rhs=self.OG[:, u, t0:t1], start=(u == 0), stop=(u == n_u - 1))
                    return ins
                Pg.op("pe", fn, reads=[slot] + self.OGs[0:n_u], writes=[bk])
                gate = self.MOD[:, 2 * 8 + c, j:j + 1]
                Pg.op("dve", lambda e, bk=bk, c=c, w=w, gate=gate, t0=t0, t1=t1: e.scalar_tensor_tensor(
                    out=self.XT[:, c, t0:t1], in0=bk[:, 0:w], scalar=gate, in1=self.XT[:, c, t0:t1], op0=ALU.mult, op1=ALU.add),
                    reads=[bk, self.MOD, self.XTb[b]], writes=[self.XTb[b]])

    def ffn(self, li, blocks):
        Pg = self.P
        ocw = self.par(132, lambda inp, core: np.ascontiguousarray(
            inp["ffn_conv_w"][li].reshape(3, 44, 128).transpose(2, 0, 1).reshape(128, 132)))
        ocb = self.par(44, lambda inp, core: fmaj(inp["ffn_conv_b"][li]))
        parts = [(0, 6), (6, 12), (12, 18), (18, 22)]
        self.ypair = 0

        def up(jj, meng="pool"):
            s = jj % 8
            self.ws.close_tile()
            ua = self.wunit(1024, lambda inp, jj=jj: fm_unit(inp["ffn_up"][li][:, jj * 128:(jj + 1) * 128]))
            ug = self.wunit(1024, lambda inp, jj=jj: fm_unit(inp["ffn_up"][li][:, DFF + jj * 128: DFF + (jj + 1) * 128]), keep=True)
            prev = None
            for b in blocks:
                t0, t1 = TB[b]
                w = t1 - t0
                bks, ys = [], []
                for hi, (slot, off) in enumerate((ua, ug)):
                    bk = self.bank("f")

                    def fn(e, bk=bk, slot=slot, off=off, w=w, t0=t0, t1=t1):
                        for kc in range(8):
                            ins = e.matmul(bk[:, 0:w], lhsT=slot[:, off + kc * 128: off + (kc + 1) * 128],
                                           rhs=self.HT[:, kc, t0:t1], start=(kc == 0), stop=(kc == 7))
                        return ins
                    Pg.op("pe", fn, reads=[slot, self.HTb[b]], writes=[bk])
                    bks.append(bk)
                yp = self.ypair % 3
                self.ypair += 1
                cw = []
                for hi in range(2):
                    bk = bks[hi]
                    ch = jj + 22 * hi
                    y = self.tmps[2 * yp + hi]
                    ya = self.tap(y)
                    w0 = self.PAR[:, ocw + 0 * 44 + ch: ocw + 0 * 44 + ch + 1]
                    w1 = self.PAR[:, ocw + 1 * 44 + ch: ocw + 1 * 44 + ch + 1]
                    w2 = self.PAR[:, ocw + 2 * 44 + ch: ocw + 2 * 44 + ch + 1]
                    cb = self.PAR[:, ocb + ch: ocb + ch + 1]
                    cw.append((bk, y, ya, w0, w2))
                    ys.append(y)
                    Pg.op("act", lambda e, ya=ya, bk=bk, w=w, w1=w1, cb=cb: e.activation(out=ya[:, 0:w], in_=bk[:, 0:w], func=AF.Identity, bias=cb, scale=w1),
                          reads=[bk, self.PAR], writes=[y])
                for (bk, y, ya, w0, w2) in cw:
                    Pg.op("dve", lambda e, ya=ya, bk=bk, w=w, w0=w0: e.scalar_tensor_tensor(
                        out=ya[:, 1:w], in0=bk[:, 0:w - 1], scalar=w0, in1=ya[:, 1:w], op0=ALU.mult, op1=ALU.add),
                        reads=[bk, self.PAR, y], writes=[y])
                for (bk, y, ya, w0, w2) in cw:
                    Pg.op("dve", lambda e, ya=ya, bk=bk, w=w, w2=w2: e.scalar_tensor_tensor(
                        out=ya[:, 0:w - 1], in0=bk[:, 1:w], scalar=w2, in1=ya[:, 0:w - 1], op0=ALU.mult, op1=ALU.add),
                        reads=[bk, self.PAR, y], writes=[y])
                if prev is not None and prev[0] >= 1 and b >= 2:
                    for hi, (bk, y, ya, w0, w2) in enumerate(cw):
                        pbk = prev[1][hi]
                        Pg.op("dve", lambda e, ya=ya, pbk=pbk, w0=w0: e.scalar_tensor_tensor(
                            out=ya[:, 0:1], in0=pbk[:, 511:512], scalar=w0, in1=ya[:, 0:1], op0=ALU.mult, op1=ALU.add),
                            reads=[pbk, self.PAR, y], writes=[y])
                    for hi, (bk, y, ya, w0, w2) in enumerate(cw):
                        py = prev[2][hi]
                        pya = self.tap(py)
                        Pg.op("dve", lambda e, pya=pya, bk=bk, w2=w2: e.scalar_tensor_tensor(
                            out=pya[:, 511:512], in0=bk[:, 0:1], scalar=w2, in1=pya[:, 511:512], op0=ALU.mult, op1=ALU.add),
                            reads=[bk, self.PAR, py], writes=[py])
                if prev is not None:
                    self.ffn_fin(prev, s, meng)
                prev = (b, bks, ys)
            self.ffn_fin(prev, s, meng)
            if li + 1 < self.nlayers:
                self.ada_tile(li + 1, jj)

        def down(j0, j1):
            nch = j1 - j0
            wd = []
            self.ws.close_tile()
            for s_ in range(nch):
                wd.append(self.wunit(1024, lambda inp, jj=j0 + s_: np.ascontiguousarray(inp["ffn_down"][li][jj * 128:(jj + 1) * 128, :]), keep=(s_ > 0)))
            wslots = []
            for (sl, _) in wd:
                if sl not in wslots:
                    wslots.append(sl)
            mslots = [(j0 + s_) % 8 for s_ in range(nch)]
            for b in blocks:
                t0, t1 = TB[b]
                w = t1 - t0
                j = 1 if b == 0 else 0
                for c in range(8):
                    bk = self.bank("g")

                    def fn(e, bk=bk, c=c, w=w, t0=t0, t1=t1, wd=wd, nch=nch, mslots=mslots):
                        for s_ in range(nch):
                            slot, off = wd[s_]
                            ins = e.matmul(bk[:, 0:w], lhsT=slot[:, off + c * 128: off + (c + 1) * 128],
                                           rhs=self.R[:, mslots[s_], t0:t1], start=(s_ == 0), stop=(s_ == nch - 1))
                        return ins
                    Pg.op("pe", fn, reads=wslots + [self.Rs[m] for m in mslots], writes=[bk])
                    gate = self.MOD[:, 5 * 8 + c, j:j + 1]
                    Pg.op("dve", lambda e, bk=bk, c=c, w=w, gate=gate, t0=t0, t1=t1: e.scalar_tensor_tensor(
                        out=self.XT[:, c, t0:t1], in0=bk[:, 0:w], scalar=gate, in1=self.XT[:, c, t0:t1], op0=ALU.mult, op1=ALU.add),
                        reads=[bk, self.MOD, self.XTb[b]], writes=[self.XTb[b]])

        tasks = []
        for pi, (j0, j1) in enumerate(parts):
            for jj in range(j0, j1):
                tasks.append(("up", jj))
                if pi > 0 and jj == j0 + 1:
                    tasks.append(("down", parts[pi - 1]))
        tasks.append(("down", parts[-1]))
        hoisted = set()
        for pi, (j0, j1) in enumerate(parts):
            if pi > 0:
                hoisted.update((j0, j0 + 1))
        hoisted.update((parts[-1][1] - 2, parts[-1][1] - 1))
        for t in tasks:
            if t[0] == "up":
                up(t[1], "dve" if t[1] in hoisted else "pool")
            else:
                down(*t[1])

    def ffn_fin(self, prev, s, eng="pool"):
        Pg = self.P
        b, bks, ys = prev
        t0, t1 = TB[b]
        w = t1 - t0
        ya, yg = self.tap(ys[0]), self.tap(ys[1])
        dst = self.R[:, s, t0:t1]
        Pg.op("act", lambda e: e.activation(out=ya[:, 0:w], in_=ya[:, 0:w], func=AF.Silu), reads=[ys[0]], writes=[ys[0]])
        Pg.op(eng, lambda e: e.tensor_tensor(out=dst, in0=ya[:, 0:w], in1=yg[:, 0:w], op=ALU.mult),
              reads=[ys[0], ys[1]], writes=[self.Rs[s]])

    def pipeline(self, n, proj, attn, wo):
        proj(0)
        for h in range(n):
            attn(h)
            if h + 1 < n:
                proj(h + 1)
            wo(h)

    def layer(self, li):
        kind = li % 4
        need_ctx = li < DEPTH - 1
        self.mark("L%d ada" % li)
        if li == 0:
            for m2 in range(18, 24):
                self.ada_tile(li, m2)
        self.ada_finish(li)
        self.mark("L%d mod1" % li)
        self.modulate(0, range(5))
        self.mark("L%d mixer" % li)
        qblocks = list(range(5)) if need_ctx else list(range(1, 5))
        Rs = self.Rs
        Pg = self.P
        ht_src = lambda t0, t1: [self.HT[:, kc, t0:t1] for kc in range(8)]
        ht_tile = lambda kc, tt: self.HT[:, kc, tt * 128:(tt + 1) * 128]
        full_ones = self.cbf[:, 0, :]

        if kind == 0:
            W = lambda inp: inp["a_w_qkv"][0]
            sw = swap_idx(128, 64)
            ol = self.par(256, lambda inp, core: np.ascontiguousarray(np.broadcast_to(np.concatenate(
                [inp["a_lambda_q1"][0], inp["a_lambda_k1"][0], inp["a_lambda_q2"][0], inp["a_lambda_k2"][0]])[None, :], (128, 256))))
            osg = self.par(1, lambda inp, core: np.ascontiguousarray(inp["a_subln_g"][0].reshape(128, 1)))
            lam_init = 0.8 - 0.6 * math.exp(-0.3 * li)
            SM = self.SM
            tm = self.tmpt()
            tma = self.tap(tm)
            Pg.op("dve", lambda e: e.tensor_tensor(out=tma[:, 0:64], in0=self.PAR[:, ol:ol + 64], in1=self.PAR[:, ol + 64:ol + 128], op=ALU.mult),
                  reads=[self.PAR], writes=[tm])
            Pg.op("dve", lambda e: e.tensor_tensor(out=tma[:, 64:128], in0=self.PAR[:, ol + 128:ol + 192], in1=self.PAR[:, ol + 192:ol + 256], op=ALU.mult),
                  reads=[self.PAR, tm], writes=[tm])
            Pg.op("dve", lambda e: e.tensor_reduce(out=SM[:, 0:2], in_=tma[:, 0:128].rearrange("p (a f) -> p a f", a=2), axis=mybir.AxisListType.X, op=ALU.add),
                  reads=[tm], writes=[SM])
            Pg.op("act", lambda e: e.activation(out=SM[:, 2:4], in_=SM[:, 0:2], func=AF.Exp), reads=[SM], writes=[SM])
            Pg.op("dve", lambda e: e.tensor_tensor(out=SM[:, 4:5], in0=SM[:, 3:4], in1=SM[:, 2:3], op=ALU.subtract), reads=[SM], writes=[SM])
            Pg.op("dve", lambda e: e.tensor_scalar(out=SM[:, 8:9], in0=SM[:, 4:5], scalar1=-lam_init, scalar2=None, op0=ALU.add), reads=[SM], writes=[SM])
            Pg.op("dve", lambda e: e.tensor_scalar(out=SM[:, 9:10], in0=self.PAR[:, osg:osg + 1], scalar1=(1.0 - lam_init), scalar2=None, op0=ALU.mult),
                  reads=[SM, self.PAR], writes=[SM])
            Pg.op("pool", lambda e: e.memset(self.R[64:128, 3, :], 0.0), writes=[Rs[3]])
            Pg.op("pool", lambda e: e.memset(self.R[0:64, 4, :], 0.0), writes=[Rs[4]])
            def proj(h):
                self.ws.close_tile()
                self.proj_fm(Rs[0], ht_src, self.HTb, 8, lambda inp, h=h: fm_unit(W(inp)[:, 128 * h:128 * h + 128]),
                             lambda inp, h=h: fm_unit(W(inp)[:, 128 * h + sw]), "rope", self.tab64_d)
                self.proj_fm(Rs[3], ht_src, self.HTb, 8, lambda inp, h=h: fm_unit(W(inp)[:, 1024 + 128 * h:1024 + 128 * h + 128]),
                             lambda inp, h=h: fm_unit(W(inp)[:, 1024 + 128 * h + sw]), "rope", self.tab64_d, dst2=Rs[4])
                self.proj_tm(Rs[6], ht_tile, self.HTb, 8, lambda inp, h=h: fm_unit(W(inp)[:, 2048 + 128 * h:2048 + 128 * h + 128]))

            def attn(h):
                units = [dict(mode="diff", og=0, maps=[
                    dict(terms=[(Rs[3 + jm], 3 + jm, Rs[0], 0)], v=(Rs[6], 6), ones=full_ones) for jm in range(2)])]
                self.attention(units, 64 ** -0.5, qblocks)

            def wo(h):
                self.wo_apply(li, 1, lambda inp, h=h: np.ascontiguousarray(inp["a_w_o"][0][128 * h:128 * h + 128, :]), qblocks)
            self.pipeline(8, proj, attn, wo)
        elif kind == 1:
            W = lambda inp: inp["b_w_qkv"][0]
            sw = swap_idx(128, 64)
            osk = self.par(8, lambda inp, core: np.ascontiguousarray(
                inp["b_sink"][0][(2 * np.arange(8)[None, :] + (np.arange(128)[:, None] >= 64)).astype(np.int64)]))
            Pg.op("act", lambda e: e.activation(out=self.SM[:, 16:24], in_=self.PAR[:, osk:osk + 8], func=AF.Exp), reads=[self.PAR], writes=[self.SM])
            zc = np.zeros((1024, 64), np.float32)
            Pg.op("pool", lambda e: e.memset(self.R[:, 6, :], 0.0), writes=[Rs[6]])
            Pg.op("pool", lambda e: e.memset(self.R[:, 7, :], 0.0), writes=[Rs[7]])
            Pg.op("pool", lambda e: e.memset(self.R[64:128, 3, :], 0.0), writes=[Rs[3]])
            Pg.op("pool", lambda e: e.memset(self.R[0:64, 4, :], 0.0), writes=[Rs[4]])
            def proj(kv):
                self.ws.close_tile()
                for c in range(2):
                    c0 = 256 * kv + 128 * c
                    self.proj_fm(Rs[c], ht_src, self.HTb, 8, lambda inp, c0=c0: fm_unit(W(inp)[:, c0:c0 + 128]),
                                 lambda inp, c0=c0: fm_unit(W(inp)[:, c0 + sw]), "rope", self.tab64_d)
                kcols = 1024 + 64 * kv + (np.arange(128) % 64)
                self.proj_fm(Rs[3], ht_src, self.HTb, 8, lambda inp, kcols=kcols: fm_unit(W(inp)[:, kcols]),
                             lambda inp, kcols=kcols: fm_unit(W(inp)[:, kcols[sw]]), "rope", self.tab64_d, dst2=Rs[4])
                v0 = 1280 + 64 * kv
                self.proj_tm(Rs[6], ht_tile, self.HTb, 8, lambda inp, v0=v0: fm_unit(np.concatenate([W(inp)[:, v0:v0 + 64], W(inp)[:, v0:v0 + 64]], axis=1)),
                             dst2=Rs[7])

            def attn(kv):
                units = []
                for c in range(2):
                    units.append(dict(mode="sum", og=c, gch=2 * kv + c, maps=[
                        dict(terms=[(Rs[3 + r], 3 + r, Rs[c], c)], v=(Rs[6 + r], 6 + r), ones=self.cbf[:, 1 + r, :]) for r in range(2)]))
                self.attention(units, 64 ** -0.5, qblocks, window=True, zadd=lambda u: self.SM[:, 16 + u["gch"]:17 + u["gch"]])

            def wo(kv):
                self.wo_apply(li, 2, lambda inp, kv=kv: np.ascontiguousarray(
                    inp["b_w_o"][0][256 * kv:256 * kv + 256, :].reshape(2, 128, 1024).transpose(1, 0, 2).reshape(128, 2048)), qblocks)
            self.pipeline(4, proj, attn, wo)
        elif kind == 2:
            Wd_ = lambda inp: inp["c_w_down"][0]
            sw = swap_idx(128, 64)
            oq = self.par(3, lambda inp, core: fmaj(inp["c_q_norm_g"][0]))
            okv = self.par(2, lambda inp, core: fmaj(inp["c_kv_norm_g"][0]))
            self.ws.close_tile()
            krc = 640 + (np.arange(128) % 64)
            recs = [lambda inp, c=c: fm_unit(Wd_(inp)[:, 128 * c:128 * c + 128]) for c in range(5)]
            recs.append(lambda inp: fm_unit(Wd_(inp)[:, krc]))
            recs.append(lambda inp: fm_unit(Wd_(inp)[:, krc[sw]]))
            wu = [self.wunit(1024, r, keep=(i > 0)) for i, r in enumerate(recs)]
            for b in range(5):
                t0, t1 = TB[b]
                w = t1 - t0
                nb = 7 if b > 0 else 6
                bks = [self.banks[i] for i in range(nb)]
                for i in range(nb):
                    slot, off = wu[i]

                    def fn(e, i=i, slot=slot, off=off, w=w, t0=t0, t1=t1):
                        for kc in range(8):
                            ins = e.matmul(self.banks[i][:, 0:w], lhsT=slot[:, off + kc * 128: off + (kc + 1) * 128],
                                           rhs=self.HT[:, kc, t0:t1], start=(kc == 0), stop=(kc == 7))
                        return ins
                    Pg.op("pe", fn, reads=[slot, self.HTb[b]], writes=[bks[i]])
                for (cs, nf, og_) in (((0, 1, 2), 384, oq), ((3, 4), 256, okv)):
                    r = self.rms_stat([bks[c][:, 0:w] for c in cs], w, nf, [bks[c] for c in cs])
                    rap = self.tap(r)[:, 0:w]
                    for ci, c in enumerate(cs):
                        g = self.PAR[:, og_ + ci: og_ + ci + 1]
                        Pg.op("dve", lambda e, c=c, g=g, rap=rap, w=w, t0=t0, t1=t1: e.scalar_tensor_tensor(
                            out=self.HT[:, c, t0:t1], in0=self.banks[c][:, 0:w], scalar=g, in1=rap, op0=ALU.mult, op1=ALU.mult),
                            reads=[bks[c], self.PAR, r], writes=[self.HTb[b]])
                dsta = self.R[:, 5, t0:t1]
                if b == 0:
                    Pg.op("act", lambda e, dsta=dsta, w=w: e.activation(out=dsta, in_=self.banks[5][:, 0:w], func=AF.Copy),
                          reads=[bks[5]], writes=[Rs[5]])
                else:
                    TAB = self.load_tab(self.tab64_d, b)
                    ta, tb_ = self.tmpt(), self.tmpt()
                    Pg.op("dve", lambda e, TAB=TAB, ta=ta, w=w: e.tensor_tensor(out=self.tap(ta)[:, 0:w], in0=self.banks[5][:, 0:w], in1=TAB[:, 0, 0:w], op=ALU.mult),
                          reads=[bks[5], TAB], writes=[ta])
                    Pg.op("dve", lambda e, TAB=TAB, tb_=tb_, w=w: e.tensor_tensor(out=self.tap(tb_)[:, 0:w], in0=self.banks[6][:, 0:w], in1=TAB[:, 1, 0:w], op=ALU.mult),
                          reads=[bks[6], TAB], writes=[tb_])
                    Pg.op("pool", lambda e, ta=ta, tb_=tb_, dsta=dsta, w=w: e.tensor_tensor(out=dsta, in0=self.tap(ta)[:, 0:w], in1=self.tap(tb_)[:, 0:w], op=ALU.add),
                          reads=[ta, tb_], writes=[Rs[5]])
            Wq = lambda inp: inp["c_w_uq"][0]
            Wkv = lambda inp: inp["c_w_ukv"][0]
            cq_src = lambda t0, t1: [self.HT[:, kc, t0:t1] for kc in range(3)]
            ckv_src = lambda t0, t1: [self.HT[:, 3 + kc, t0:t1] for kc in range(2)]
            ckv_tile = lambda kc, tt: self.HT[:, 3 + kc, tt * 128:(tt + 1) * 128]
            zq = np.zeros((384, 64), np.float32)
            def proj(h):
                self.ws.close_tile()
                self.proj_fm(Rs[0], cq_src, self.HTb, 3, lambda inp, h=h: fm_unit(Wq(inp)[:, 192 * h:192 * h + 128]), None, "copy")
                rc = 192 * h + 128 + np.arange(64)
                self.proj_fm(Rs[1], cq_src, self.HTb, 3, lambda inp, rc=rc: fm_unit(np.concatenate([Wq(inp)[:, rc], zq], axis=1)),
                             lambda inp, rc=rc: fm_unit(np.concatenate([Wq(inp)[:, rc[swap_idx(64, 64)]], zq], axis=1)), "rope", self.tab64_d)
                self.proj_fm(Rs[3], ckv_src, self.HTb, 2, lambda inp, h=h: fm_unit(Wkv(inp)[:, 256 * h:256 * h + 128]), None, "copy")
                self.proj_tm(Rs[6], ckv_tile, self.HTb, 2, lambda inp, h=h: fm_unit(Wkv(inp)[:, 256 * h + 128:256 * h + 256]))

            def attn(h):
                units = [dict(mode="sum", og=0, maps=[dict(
                    terms=[(Rs[3], 3, Rs[0], 0), (Rs[5], 5, Rs[1], 1)], v=(Rs[6], 6), ones=full_ones)])]
                self.attention(units, 192 ** -0.5, qblocks)

            def wo(h):
                self.wo_apply(li, 1, lambda inp, h=h: np.ascontiguousarray(inp["c_w_o"][0][128 * h:128 * h + 128, :]), qblocks)
            self.pipeline(8, proj, attn, wo)
        else:
            W = lambda inp: inp["d_w_qkv"][0]
            sw = swap_idx(128, 128)
            ogq = self.par(1, lambda inp, core: np.ascontiguousarray(inp["d_q_norm_g"][0].reshape(128, 1)))
            ogqs = self.par(1, lambda inp, core: np.ascontiguousarray(inp["d_q_norm_g"][0][sw].reshape(128, 1)))
            ogk = self.par(1, lambda inp, core: np.ascontiguousarray(inp["d_k_norm_g"][0].reshape(128, 1)))
            ogks = self.par(1, lambda inp, core: np.ascontiguousarray(inp["d_k_norm_g"][0][sw].reshape(128, 1)))
            def proj(kv):
                self.ws.close_tile()
                for i in range(2):
                    c0 = 128 * (2 * kv + i)
                    self.proj_fm(Rs[i], ht_src, self.HTb, 8, lambda inp, c0=c0: fm_unit(W(inp)[:, c0:c0 + 128]),
                                 lambda inp, c0=c0: fm_unit(W(inp)[:, c0 + sw]), "normrope", self.tab128_d, gpar=ogq, gspar=ogqs)
                k0 = 1024 + 128 * kv
                self.proj_fm(Rs[3], ht_src, self.HTb, 8, lambda inp, k0=k0: fm_unit(W(inp)[:, k0:k0 + 128]),
                             lambda inp, k0=k0: fm_unit(W(inp)[:, k0 + sw]), "normrope", self.tab128_d, gpar=ogk, gspar=ogks)
                v0 = 1536 + 128 * kv
                self.proj_tm(Rs[6], ht_tile, self.HTb, 8, lambda inp, v0=v0: fm_unit(W(inp)[:, v0:v0 + 128]))

            def attn(kv):
                units = [dict(mode="sum", og=i, maps=[dict(terms=[(Rs[3], 3, Rs[i], i)], v=(Rs[6], 6), ones=full_ones)])
                         for i in range(2)]
                self.attention(units, 128 ** -0.5, qblocks)

            def wo(kv):
                self.wo_apply(li, 2, lambda inp, kv=kv: np.ascontiguousarray(
                    inp["d_w_o"][0][256 * kv:256 * kv + 256, :].reshape(2, 128, 1024).transpose(1, 0, 2).reshape(128, 2048)), qblocks)
            self.pipeline(4, proj, attn, wo)
        self.mark("L%d mod2" % li)
        self.modulate(1, qblocks)
        self.mark("L%d ffn" % li)
        self.ffn(li, qblocks)
        if li + 1 < self.nlayers:
            for m2 in (22, 23):
                self.ada_tile(li + 1, m2)

    def final(self):
        Pg = self.P
        ofg = self.par(8, lambda inp, core: fmaj(inp["final_g"]))
        stage = self.TMP.t
        for b in range(1, 5):
            t0, t1 = TB[b]
            w = t1 - t0
            srcs = [self.XT[:, c, t0:t1] for c in range(8)]
            r = self.rms_stat(srcs, w, D, [self.XTb[b]])
            rap = self.tap(r)[:, 0:w]
            for c in range(8):
                g = self.PAR[:, ofg + c: ofg + c + 1]
                Pg.op("dve", lambda e, c=c, g=g, rap=rap, t0=t0, t1=t1: e.scalar_tensor_tensor(
                    out=self.XT[:, c, t0:t1], in0=self.XT[:, c, t0:t1], scalar=g, in1=rap, op0=ALU.mult, op1=ALU.mult),
                    reads=[self.XTb[b], self.PAR, r], writes=[self.XTb[b]])
            for ti in range(4):
                tt0 = t0 + ti * 128
                so = 2 * (ti % 2)
                sb = [self.tmps[so], self.tmps[so + 1]]
                for g2 in range(2):
                    bk = self.bank("g")

                    def tr(e, bk=bk, g2=g2, tt0=tt0):
                        for i in range(4):
                            c = 4 * g2 + i
                            ins = e.transpose(out=bk[:, i * 128:(i + 1) * 128], in_=self.XT[:, c, tt0:tt0 + 128], identity=self.ident[:, :])
                        return ins
                    Pg.op("pe", tr, reads=[self.XTb[b], self.ident], writes=[bk])
                    dst = stage[:, so + g2, :]
                    if g2 == 0:
                        Pg.op("act", lambda e, dst=dst, bk=bk: e.activation(out=dst, in_=bk[:, :], func=AF.Copy), reads=[bk], writes=[sb[g2]])
                    else:
                        Pg.op("dve", lambda e, dst=dst, bk=bk: e.tensor_copy(out=dst, in_=bk[:, :]), reads=[bk], writes=[sb[g2]])
                row0 = tt0 - NCTX
                Pg.dma("sp", "xout%d" % (ti % 2), lambda e, row0=row0, so=so: e.dma_start(out=self.out_d[row0:row0 + 128, :].rearrange("p (a f) -> p a f", a=2),
                                                                                       in_=stage[:, so:so + 2, :]), reads=sb)
        Pg.wait_all("sp", [self.tmps[0], self.tmps[1], self.tmps[2], self.tmps[3]])


_CACHE = {}


def _build(nlayers=DEPTH):
    if nlayers in _CACHE:
        return _CACHE[nlayers]
    nc0 = bass.Bass("TRN2", target_bir_lowering=False)
    g0 = Gen(nc0, 100000, nlayers)
    g0.dry = True
    g0.P.emit = lambda: None
    g0.build()
    n_tiles = len(g0.ws.tiles)
    nc = bass.Bass("TRN2", target_bir_lowering=False)
    g = Gen(nc, n_tiles, nlayers)
    g.build()
    assert len(g.ws.tiles) == n_tiles
    assert g.par_cols <= g.PARW, g.par_cols
    _CACHE[nlayers] = (nc, g)
    return nc, g


def _consts():
    cb = np.zeros((128, 7, 128), np.float32)
    for dh, idx in ((64, 5), (128, 6)):
        sw_ = swap_idx(128, dh)
        cb[sw_, idx, np.arange(128)] = 1.0
    cb[:, 0, :] = 1.0
    cb[:, 1, 0:64] = 1.0
    cb[:, 2, 64:128] = 1.0
    b = np.arange(128)[:, None]
    a = np.arange(128)[None, :]
    cb[:, 3, :] = (a <= b)
    cb[:, 4, :] = (a >= b)
    return cb.astype(ml_dtypes.bfloat16)


def make_in_maps(inp, g, cores):
    wstream = g.ws.materialize(inp)
    tab64 = rope_tables(64)
    tab128 = rope_tables(128)
    ident = np.eye(128, dtype=np.float32)
    cbf = _consts()
    in_maps = []
    for core in cores:
        par = np.zeros((128, g.PARW), np.float32)
        for off, n, rec in g.par_recipes:
            par[:, off:off + n] = rec(inp, core)
        in_maps.append({
            "x": np.ascontiguousarray(inp["x"][core]),
            "ctx": np.ascontiguousarray(inp["ctx"][core]),
            "wstream": wstream,
            "tab64": tab64,
            "tab128": tab128,
            "ident": ident,
            "cbf": cbf,
            "par": par,
        })
    return in_maps


def kernel(nlayers=DEPTH, **inputs):
    inp = {k: np.asarray(v) for k, v in inputs.items()}
    nc, g = _build(nlayers)
    in_maps = make_in_maps(inp, g, range(8))
    res = run_bass_kernel_spmd(nc, in_maps, core_ids=list(range(8)))
    out = np.stack([np.asarray(r["out"]) for r in res.results], axis=0).astype(np.float32)
    return out
```
